# Optimizing a Trainium2 kernel written in Bass

```python
import math
import jax, jax.numpy as jnp
from jax import lax
import numpy as np

D_MODEL = 1024
BATCH = 8
SEQ = 2048
DEPTH = 1

CHUNK = 64
LEFT_CHUNKS = 8
BAND = (LEFT_CHUNKS + 1) * CHUNK
Q_BLOCK = 128
EPS = 1e-6
NEG = -1e30

A_HEADS = 8
A_HEAD_DIM = 64
A_WIDTH = A_HEADS * A_HEAD_DIM
REL_CLIP = 128

B_HEADS = 4
B_HEAD_DIM = 64
B_V_DIM = 2 * B_HEAD_DIM
B_QK_WIDTH = B_HEADS * 2 * B_HEAD_DIM
B_WIDTH = B_HEADS * B_V_DIM
T5_BUCKETS = 32
T5_MAX_DIST = 128

N_BRANCH = 2
D_FF = 2816
MIX_IN = 3 * A_WIDTH + 2 * B_QK_WIDTH + B_WIDTH + N_BRANCH * D_MODEL
SPLITS = list(np.cumsum([A_WIDTH, A_WIDTH, A_WIDTH, B_QK_WIDTH, B_QK_WIDTH, B_WIDTH]).tolist())

kernel_name = "hybrid_chunk_band_diff_attn_gated_macaron"


def rms_norm(x, g):
    xf = x.astype(jnp.float32)
    y = xf * lax.rsqrt(jnp.mean(xf * xf, axis=-1, keepdims=True) + EPS)
    return (y * g.astype(jnp.float32)).astype(x.dtype)


def swiglu_ffn(x, w_in, w_out):
    gate, up = jnp.split(x @ w_in, 2, axis=-1)
    return (jax.nn.silu(gate) * up) @ w_out


def t5_bucket(rel):
    nb = T5_BUCKETS // 2
    ret = jnp.where(rel > 0, nb, 0)
    n = jnp.abs(rel)
    max_exact = nb // 2
    nf = jnp.maximum(n, 1).astype(jnp.float32)
    large = max_exact + (jnp.log(nf / max_exact) / math.log(T5_MAX_DIST / max_exact)
                         * (nb - max_exact)).astype(jnp.int32)
    large = jnp.minimum(large, nb - 1)
    return ret + jnp.where(n < max_exact, n, large)


def chunked_band_attention(q, k, v, rel_table):
    b, s, h, d = q.shape
    nc = s // CHUNK
    qc = q.reshape(b, nc, CHUNK, h, d)
    pad = ((0, 0), (LEFT_CHUNKS * CHUNK, 0), (0, 0), (0, 0))
    kp = jnp.pad(k, pad).reshape(b, nc + LEFT_CHUNKS, CHUNK, h, d)
    vp = jnp.pad(v, pad).reshape(b, nc + LEFT_CHUNKS, CHUNK, h, d)
    kb = jnp.concatenate([kp[:, i:i + nc] for i in range(LEFT_CHUNKS + 1)], axis=2)
    vb = jnp.concatenate([vp[:, i:i + nc] for i in range(LEFT_CHUNKS + 1)], axis=2)
    sc = jnp.einsum('bcqhd,bckhd->bhcqk', qc, kb).astype(jnp.float32) * (d ** -0.5)
    qi = jnp.arange(CHUNK)[:, None]
    kj = jnp.arange(BAND)[None, :]
    rel = kj - LEFT_CHUNKS * CHUNK - qi
    bias = rel_table[:, jnp.clip(rel, -REL_CLIP, REL_CLIP) + REL_CLIP]
    sc = sc + bias[:, None].astype(jnp.float32)
    key_pos = (jnp.arange(nc)[:, None] - LEFT_CHUNKS) * CHUNK + jnp.arange(BAND)[None, :]
    sc = jnp.where((key_pos >= 0)[None, None, :, None, :], sc, NEG)
    p = jax.nn.softmax(sc, axis=-1).astype(v.dtype)
    o = jnp.einsum('bhcqk,bckhd->bcqhd', p, vb)
    return o.reshape(b, s, h * d)


def diff_attention(q, k, v, t5_table, lam, lambda_init, subln_g):
    b, s, h, _, d = q.shape
    nqb = s // Q_BLOCK
    qblocks = q.reshape(b, nqb, Q_BLOCK, h, 2, d).transpose(1, 0, 2, 3, 4, 5)
    key_pos = jnp.arange(s)
    key_chunk = key_pos // CHUNK
    scale = d ** -0.5

    def block(args):
        qblk, bi = args
        q_pos = bi * Q_BLOCK + jnp.arange(Q_BLOCK)
        sc = jnp.einsum('bqhrd,bkhrd->bhrqk', qblk, k).astype(jnp.float32) * scale
        bias = t5_table[:, t5_bucket(key_pos[None, :] - q_pos[:, None])]
        sc = sc + bias[None, :, None].astype(jnp.float32)
        allowed = key_chunk[None, :] <= (q_pos // CHUNK)[:, None]
        sc = jnp.where(allowed, sc, NEG)
        p = jax.nn.softmax(sc, axis=-1)
        attn = p[:, :, 0] - lam * p[:, :, 1]
        return jnp.einsum('bhqk,bkhe->bqhe', attn.astype(v.dtype), v)

    o = lax.map(block, (qblocks, jnp.arange(nqb)))
    o = o.transpose(1, 0, 2, 3, 4).reshape(b, s, h, 2 * d)
    o = rms_norm(o, subln_g) * (1.0 - lambda_init)
    return o.reshape(b, s, h * 2 * d)


def setup_inputs(seed: int = 0) -> dict:
    key = jax.random.key(seed)
    ks = jax.random.split(key, 24)

    def dense(k, shape, fan_in):
        return jax.random.normal(k, shape, jnp.float32) * (fan_in ** -0.5)

    def gain(k, shape):
        return 1.0 + 0.05 * jax.random.normal(k, shape, jnp.float32)

    L = DEPTH
    return {
        "x": jax.random.normal(ks[0], (BATCH, SEQ, D_MODEL), jnp.float32),
        "ffn1_norm": gain(ks[1], (L, D_MODEL)),
        "ffn1_w_in": dense(ks[2], (L, D_MODEL, 2 * D_FF), D_MODEL),
        "ffn1_w_out": dense(ks[3], (L, D_FF, D_MODEL), D_FF),
        "mix_norm": gain(ks[4], (L, D_MODEL)),
        "w_mix_in": dense(ks[5], (L, D_MODEL, MIX_IN), D_MODEL),
        "b_gate": 0.02 * jax.random.normal(ks[6], (L, N_BRANCH * D_MODEL), jnp.float32),
        "rel_bias_a": 0.1 * jax.random.normal(ks[7], (L, A_HEADS, 2 * REL_CLIP + 1), jnp.float32),
        "lambda_q1": 0.1 * jax.random.normal(ks[8], (L, B_HEAD_DIM), jnp.float32),
        "lambda_k1": 0.1 * jax.random.normal(ks[9], (L, B_HEAD_DIM), jnp.float32),
        "lambda_q2": 0.1 * jax.random.normal(ks[10], (L, B_HEAD_DIM), jnp.float32),
        "lambda_k2": 0.1 * jax.random.normal(ks[11], (L, B_HEAD_DIM), jnp.float32),
        "subln_g": gain(ks[12], (L, B_V_DIM)),
        "t5_bias": 0.1 * jax.random.normal(ks[13], (B_HEADS, T5_BUCKETS), jnp.float32),
        "w_branch_a": dense(ks[14], (L, A_WIDTH, D_MODEL), A_WIDTH),
        "w_branch_b": dense(ks[15], (L, B_WIDTH, D_MODEL), B_WIDTH),
        "w_o": dense(ks[16], (L, D_MODEL, D_MODEL), D_MODEL),
        "ffn2_norm": gain(ks[17], (L, D_MODEL)),
        "ffn2_w_in": dense(ks[18], (L, D_MODEL, 2 * D_FF), D_MODEL),
        "ffn2_w_out": dense(ks[19], (L, D_FF, D_MODEL), D_FF),
        "final_norm": gain(ks[20], (D_MODEL,)),
    }


def reference(x, ffn1_norm, ffn1_w_in, ffn1_w_out, mix_norm, w_mix_in, b_gate, rel_bias_a,
              lambda_q1, lambda_k1, lambda_q2, lambda_k2, subln_g, t5_bias,
              w_branch_a, w_branch_b, w_o, ffn2_norm, ffn2_w_in, ffn2_w_out, final_norm):
    b, s, _ = x.shape
    for li in range(DEPTH):
        x = x + 0.5 * swiglu_ffn(rms_norm(x, ffn1_norm[li]), ffn1_w_in[li], ffn1_w_out[li])

        u = rms_norm(x, mix_norm[li])
        proj = u @ w_mix_in[li]
        q_a, k_a, v_a, q_b, k_b, v_b, gate_logits = jnp.split(proj, SPLITS, axis=-1)

        y_a = chunked_band_attention(q_a.reshape(b, s, A_HEADS, A_HEAD_DIM),
                                     k_a.reshape(b, s, A_HEADS, A_HEAD_DIM),
                                     v_a.reshape(b, s, A_HEADS, A_HEAD_DIM),
                                     rel_bias_a[li])

        lambda_init = 0.8 - 0.6 * math.exp(-0.3 * li)
        lam = (jnp.exp(jnp.sum(lambda_q1[li].astype(jnp.float32) * lambda_k1[li].astype(jnp.float32)))
               - jnp.exp(jnp.sum(lambda_q2[li].astype(jnp.float32) * lambda_k2[li].astype(jnp.float32)))
               + lambda_init)
        y_b = diff_attention(q_b.reshape(b, s, B_HEADS, 2, B_HEAD_DIM),
                             k_b.reshape(b, s, B_HEADS, 2, B_HEAD_DIM),
                             v_b.reshape(b, s, B_HEADS, B_V_DIM),
                             t5_bias, lam, lambda_init, subln_g[li])

        g_a, g_b = jnp.split(jax.nn.sigmoid(gate_logits + b_gate[li]), 2, axis=-1)
        merged = g_a * (y_a @ w_branch_a[li]) + g_b * (y_b @ w_branch_b[li])
        x = x + merged @ w_o[li]

        x = x + 0.5 * swiglu_ffn(rms_norm(x, ffn2_norm[li]), ffn2_w_in[li], ffn2_w_out[li])
    return rms_norm(x, final_norm)
```

```python
import contextlib
import math
import os

MIXDBG = int(os.environ.get('MIXDBG', '99'))
STRICT = os.environ.get('KSTRICT', '1') == '1'

import numpy as np

import concourse.bass as bass
import concourse.mybir as mybir
from concourse.bass_utils import run_bass_kernel_spmd

F32 = mybir.dt.float32
BF16 = mybir.dt.bfloat16
AF = mybir.ActivationFunctionType
ALU = mybir.AluOpType

D = 1024
S = 2048
DFF = 2816
NCH = 22
NHP = 11
EPS = 1e-6
MASKV = -30000.0
LAMBDA_INIT = 0.8 - 0.6 * math.exp(0.0)

PC_G1, PC_GM, PC_G2, PC_GF, PC_BG = 0, 8, 16, 24, 32
PC_LQ1, PC_LK1, PC_LQ2, PC_LK2 = 48, 112, 176, 240
PC_SUBG = 304
PC_T5C = 432
NPAR = 436

ENG_NAMES = ("pe", "act", "dve", "pool", "sp")


class _Op:
    __slots__ = ("eng", "emit", "deps", "is_dma", "sig", "need_sig")

    def __init__(self, eng, emit, is_dma):
        self.eng = eng
        self.emit = emit
        self.is_dma = is_dma
        self.deps = []
        self.sig = None
        self.need_sig = False


class Prog:
    def __init__(self):
        self.ops = {e: [] for e in ENG_NAMES}
        self.last_writer = {}
        self.readers = {}
        self.dma_counts = {}

    def _add(self, o, reads, writes):
        deps = {}
        for k in reads:
            lw = self.last_writer.get(k)
            if lw is not None:
                deps[id(lw)] = (lw, True)
        for k in writes:
            lw = self.last_writer.get(k)
            if lw is not None and id(lw) not in deps:
                deps[id(lw)] = (lw, False)
            for r in self.readers.get(k, ()):
                if id(r) not in deps:
                    deps[id(r)] = (r, False)
        for k in reads:
            self.readers.setdefault(k, []).append(o)
        for k in writes:
            self.last_writer[k] = o
            self.readers[k] = []
        for d, raw in deps.values():
            if d is o:
                continue
            if d.eng == o.eng and not d.is_dma and not o.is_dma and not raw and not STRICT:
                continue
            d.need_sig = True
            o.deps.append(d)
        self.ops[o.eng].append(o)
        return o

    def op(self, eng, emit, reads=(), writes=()):
        return self._add(_Op(eng, emit, False), reads, writes)

    def dma(self, eng, emit, semkey, n_dmas=1, reads=(), writes=()):
        o = _Op(eng, emit, True)
        c = self.dma_counts.get(semkey, 0) + 16 * n_dmas
        self.dma_counts[semkey] = c
        o.sig = (("dma", semkey), c)
        return self._add(o, reads, writes)

    def emit_all(self, nc, stack):
        sems = {}
        for e in ENG_NAMES:
            sems[("eng", e)] = stack.enter_context(nc.semaphore("s_" + e))
        for k in self.dma_counts:
            sems[("dma", k)] = stack.enter_context(nc.semaphore("d_" + str(k)))
        for e in ENG_NAMES:
            t = 0
            for o in self.ops[e]:
                if o.is_dma:
                    continue
                if o.need_sig:
                    t += 1
                    o.sig = (("eng", e), t)
        block = stack.enter_context(nc.Block())
        prog = self

        def run(e):
            def body(eng):
                waited = {}
                for o in prog.ops[e]:
                    need = {}
                    for d in o.deps:
                        sk, v = d.sig
                        if waited.get(sk, 0) >= v:
                            continue
                        if need.get(sk, 0) < v:
                            need[sk] = v
                    for sk, v in need.items():
                        eng.wait_ge(sems[sk], v)
                        waited[sk] = v
                    if o.is_dma:
                        o.emit(eng, sems[o.sig[0]])
                    else:
                        ins = o.emit(eng)
                        if o.need_sig:
                            ins.then_inc(sems[o.sig[0]], 1)
            return body

        block.tensor(run("pe"))
        block.scalar(run("act"))
        block.vector(run("dve"))
        block.gpsimd(run("pool"))
        block.sync(run("sp"))


def build_program(stages=("ffn1", "mix", "ffn2")):
    nc = bass.Bass("TRN2", target_bir_lowering=False)
    dt_in = lambda name, shape: nc.dram_tensor(name, list(shape), F32, kind="ExternalInput").ap()
    xT_d = dt_in("xT", [128, 8, S])
    par_d = dt_in("params", [128, NPAR])
    ident_d = dt_in("ident", [128, 128])
    w1i_d = dt_in("w1i", [NCH, 128, 2048])
    w1o_d = dt_in("w1o", [2, 8, 128, NHP * 128])
    w2i_d = dt_in("w2i", [NCH, 128, 2048])
    w2o_d = dt_in("w2o", [2, 8, 128, NHP * 128])
    wqkv_d = dt_in("wqkv", [12, 128, 2048])
    wg_d = dt_in("wg", [8, 128, 2048])
    wbr_d = dt_in("wbr", [8, 128, 1024])
    wo_d = dt_in("wo", [2, 2, 128, 2048])
    ba_d = dt_in("biasA", [2, 128, 2048])
    bb_d = dt_in("biasB", [2, 128, 1280])
    out_d = nc.dram_tensor("outT", [128, 8, S], F32, kind="ExternalOutput").ap()

    with contextlib.ExitStack() as st:
        sb = lambda name, shape, dt: st.enter_context(nc.sbuf_tensor(name, list(shape), dt))
        X = sb("X", [128, 8, S], F32)
        U = sb("U", [128, 8, S], BF16)
        H = sb("H", [128, 12, S], BF16)
        V = sb("V", [128, 16, 264], BF16)
        RB = sb("RB", [128, S], F32)
        NWB = 4
        WB = [sb(f"WB{i}", [128, 2048], BF16) for i in range(NWB)]
        BI = sb("BI", [128, 2048], BF16)
        NE = 4
        E = [sb(f"E{i}", [128, 512], BF16) for i in range(NE)]
        NSC = 4
        SC = [sb(f"SC{i}", [128, 512], F32) for i in range(NSC)]
        QZ = [sb(f"QZ{i}", [128, 1024], BF16) for i in range(2)]
        NYS = 5
        YS = [sb(f"YS{i}", [128, 256], BF16) for i in range(NYS)]
        NST = 5
        STG = [sb(f"STG{i}", [128, 258], F32) for i in range(NST)]
        SM = [sb(f"SM{i}", [128, 8], F32) for i in range(NST)]
        PR = sb("PR", [128, NPAR], F32)
        IDb = sb("IDb", [128, 128], BF16)
        ONESb = sb("ONESb", [128, 128], BF16)
        EPST = sb("EPST", [128, 1], F32)
        LAMT = sb("LAMT", [128, 8], F32)
        G08 = sb("G08", [128, 128], F32)
        LTMP = sb("LTMP", [128, 64], F32)
        PS = st.enter_context(nc.psum_tensor("PS", [128, 8 * 512], F32))
        PST = PS[:, 7 * 512:8 * 512].bitcast(BF16)

        P = Prog()
        bank = lambda b: PS[:, b * 512:(b + 1) * 512]
        rot = {"ps": 0, "psw": 0, "psa": 0, "psb": 0, "accb": 0, "e": 0, "sc": 0, "ys": 0, "st": 0, "qz": 0, "tp": 0}

        def nxt(name, n):
            v = rot[name]
            rot[name] = (v + 1) % n
            return v

        wplan = []
        wstate = {"emitted": 0}

        def wslot(i):
            return i % NWB

        def wneed(i, ahead=int(os.environ.get('WAHEAD', '2'))):
            hi = min(len(wplan) - 1, i + ahead)
            while wstate["emitted"] <= hi:
                L = wstate["emitted"]
                src, ncols = wplan[L]
                s = wslot(L)
                P.dma("pool", lambda e, sem, src=src, s=s, ncols=ncols:
                      e.dma_start(out=WB[s][:, 0:ncols], in_=src).then_inc(sem, 16),
                      f"wb{s}", writes=[("WB", s)])
                wstate["emitted"] += 1
            return wslot(i)

        P.dma("sp", lambda e, sem: e.dma_start(out=PR[:], in_=par_d).then_inc(sem, 16), "pr", writes=["PR"])
        for tb in range(4):
            P.dma("sp", lambda e, sem, tb=tb: e.dma_start(
                out=X[:, :, tb * 512:(tb + 1) * 512], in_=xT_d[:, :, tb * 512:(tb + 1) * 512]).then_inc(sem, 16),
                f"x{tb}", writes=[("X", c, tb) for c in range(8)])
        P.dma("pool", lambda e, sem: e.dma_start(out=IDb[:], in_=ident_d).then_inc(sem, 16), "id", writes=["ID"])
        P.op("dve", lambda e: e.memset(ONESb[:], 1.0), writes=["ONES"])
        P.op("dve", lambda e: e.memset(EPST[:], EPS), writes=["EPS"])

        def rms_norm(gcol, final=False, pre_banks=None):
            banks = pre_banks if pre_banks is not None else [nxt("ps", 7) for _ in range(4)]

            def stats(tb):
                tsl = slice(tb * 512, (tb + 1) * 512)
                for c in range(8):
                    P.op("act", lambda e, c=c: e.activation(out=H[:, c, tsl], in_=X[:, c, tsl], func=AF.Square),
                         reads=[("X", c, tb)], writes=[("H", c, tb)])

                def mm(e):
                    ins = None
                    for c in range(8):
                        ins = e.matmul(bank(banks[tb]), lhsT=ONESb[:], rhs=H[:, c, tsl], start=(c == 0), stop=(c == 7))
                    return ins
                P.op("pe", mm, reads=["ONES"] + [("H", c, tb) for c in range(8)], writes=[("ps", banks[tb])])

            def finish(tb):
                tsl = slice(tb * 512, (tb + 1) * 512)
                P.op("act", lambda e: e.activation(out=RB[:, tsl], in_=bank(banks[tb]), func=AF.Ln,
                                                   bias=EPST[:, 0:1], scale=1.0 / D),
                     reads=["EPS"], writes=[("RB", tb), ("ps", banks[tb])])
                P.op("act", lambda e: e.activation(out=RB[:, tsl], in_=RB[:, tsl], func=AF.Exp, scale=-0.5),
                     reads=[("RB", tb)], writes=[("RB", tb)])
                for c in range(8):
                    dst = X if final else U
                    P.op("dve", lambda e, c=c, dst=dst: e.scalar_tensor_tensor(
                        out=dst[:, c, tsl], in0=X[:, c, tsl], scalar=PR[:, gcol + c:gcol + c + 1], in1=RB[:, tsl],
                        op0=ALU.mult, op1=ALU.mult),
                        reads=["PR", ("RB", tb), ("X", c, tb)], writes=[(("X" if final else "U"), c, tb)])
                if final:
                    P.dma("sp", lambda e, sem: e.dma_start(out=out_d[:, :, tsl], in_=X[:, :, tsl]).then_inc(sem, 16),
                          f"o{tb}", reads=[("X", c, tb) for c in range(8)], writes=[("OUT", tb)])

            if pre_banks is not None:
                for tb in range(4):
                    finish(tb)
                return
            for tb in range(4):
                stats(tb)
                if tb >= 1:
                    finish(tb - 1)
            finish(3)

        def ffn_plan(wi, wo):
            order = []
            for hp in range(2):
                order += [(wi[hp * NHP + cc], 2048) for cc in range(NHP)]
                order += [(wo[hp, m], NHP * 128) for m in range(8)]
            return order
        plan_pos = {}
        FFNREP = int(os.environ.get("FFNREP", "0"))
        if "ffn1" in stages:
            plan_pos["ffn1"] = len(wplan); wplan += ffn_plan(w1i_d, w1o_d)
            for rep in range(FFNREP):
                plan_pos["ffn1r%d" % rep] = len(wplan); wplan += ffn_plan(w1i_d, w1o_d)
        if "mix" in stages:
            plan_pos["qkv"] = len(wplan); wplan += [(wqkv_d[i], 2048) for i in range(12)]
            plan_pos["merge"] = len(wplan)
            for mg in range(2):
                for mm_ in range(4):
                    m = mg * 4 + mm_
                    wplan += [(wg_d[m], 2048), (wbr_d[m], 1024)]
                wplan += [(wo_d[mg, 0], 2048), (wo_d[mg, 1], 2048)]
        if "ffn2" in stages:
            plan_pos["ffn2"] = len(wplan); wplan += ffn_plan(w2i_d, w2o_d)

        STATB = [3, 4, 5, 6]

        def ffn_stage(key, gcol, pre_banks=None, fuse_next=False):
            base = plan_pos[key]
            rms_norm(gcol, pre_banks=pre_banks)
            for hp in range(2):
                pbase = base + hp * (NHP + 8)
                for cc in range(NHP):
                    s = wneed(pbase + cc)
                    for tb in range(4):
                        bg = nxt("ps", 7)
                        bu = nxt("ps", 7)
                        tsl = slice(tb * 512, (tb + 1) * 512)

                        def mm(e, s=s, bg=bg, bu=bu, tsl=tsl):
                            ins = None
                            for kc in range(8):
                                e.matmul(bank(bg), lhsT=WB[s][:, kc * 256:kc * 256 + 128], rhs=U[:, kc, tsl],
                                         start=(kc == 0), stop=(kc == 7))
                            for kc in range(8):
                                ins = e.matmul(bank(bu), lhsT=WB[s][:, kc * 256 + 128:kc * 256 + 256],
                                               rhs=U[:, kc, tsl], start=(kc == 0), stop=(kc == 7))
                            return ins
                        P.op("pe", mm, reads=[("WB", s)] + [("U", kc, tb) for kc in range(8)],
                             writes=[("ps", bg), ("ps", bu)])
                        sc = nxt("sc", NSC)
                        P.op("act", lambda e, bg=bg, sc=sc: e.activation(out=SC[sc][:], in_=bank(bg), func=AF.Silu),
                             writes=[("ps", bg), ("SC", sc)])
                        P.op("dve", lambda e, bu=bu, sc=sc, cc=cc, tsl=tsl: e.tensor_tensor(
                            out=H[:, cc, tsl], in0=bank(bu), in1=SC[sc][:], op=ALU.mult),
                            reads=[("SC", sc)], writes=[("ps", bu), ("H", cc, tb)])
                fuse = fuse_next and hp == 1
                pending = []
                for m in range(8):
                    s = wneed(pbase + NHP + m)
                    for tb in range(4):
                        b = nxt("psw", 3) if fuse else nxt("ps", 7)
                        tsl = slice(tb * 512, (tb + 1) * 512)

                        def mm(e, s=s, b=b, tsl=tsl):
                            ins = None
                            for kc in range(NHP):
                                ins = e.matmul(bank(b), lhsT=WB[s][:, kc * 128:(kc + 1) * 128], rhs=H[:, kc, tsl],
                                               start=(kc == 0), stop=(kc == NHP - 1))
                            return ins
                        P.op("pe", mm, reads=[("WB", s)] + [("H", kc, tb) for kc in range(NHP)],
                             writes=[("ps", b)])
                        P.op("dve", lambda e, b=b, m=m, tsl=tsl: e.scalar_tensor_tensor(
                            out=X[:, m, tsl], in0=bank(b), scalar=0.5, in1=X[:, m, tsl],
                            op0=ALU.mult, op1=ALU.add),
                            writes=[("ps", b), ("X", m, tb)])
                        if fuse:
                            P.op("act", lambda e, m=m, tsl=tsl: e.activation(out=U[:, m, tsl], in_=X[:, m, tsl],
                                                                             func=AF.Square),
                                 reads=[("X", m, tb)], writes=[("U", m, tb)])

                            def sq_mm(m=m, tb=tb, tsl=tsl):
                                P.op("pe", lambda e: e.matmul(bank(STATB[tb]), lhsT=ONESb[:], rhs=U[:, m, tsl],
                                                              start=(m == 0), stop=(m == 7)),
                                     reads=["ONES", ("U", m, tb)], writes=[("ps", STATB[tb])])
                            pending.append(sq_mm)
                            if len(pending) > 2:
                                pending.pop(0)()
                while pending:
                    pending.pop(0)()

        def proj_fm(s, col0, hidx, evac_alt):
            for tb in range(4):
                b = nxt("ps", 7)
                tsl = slice(tb * 512, (tb + 1) * 512)

                def mm(e, b=b, tsl=tsl):
                    ins = None
                    for kc in range(8):
                        ins = e.matmul(bank(b), lhsT=WB[s][:, kc * 256 + col0:kc * 256 + col0 + 128],
                                       rhs=U[:, kc, tsl], start=(kc == 0), stop=(kc == 7))
                    return ins
                P.op("pe", mm, reads=[("WB", s)] + [("U", kc, tb) for kc in range(8)], writes=[("ps", b)])
                if (tb + evac_alt) % 2 == 0:
                    P.op("act", lambda e, b=b, tsl=tsl: e.copy(out=H[:, hidx, tsl], in_=bank(b)),
                         writes=[("ps", b), ("H", hidx, tb)])
                else:
                    P.op("dve", lambda e, b=b, tsl=tsl: e.tensor_copy(out=H[:, hidx, tsl], in_=bank(b)),
                         writes=[("ps", b), ("H", hidx, tb)])

        def proj_v(s, nh, hd):
            for tt in range(16):
                b = nxt("ps", 7)

                def mm(e, b=b, tt=tt):
                    ins = None
                    for kc in range(8):
                        ins = e.matmul(bank(b)[:, 0:256], lhsT=U[:, kc, tt * 128:(tt + 1) * 128],
                                       rhs=WB[s][:, kc * 256:(kc + 1) * 256], start=(kc == 0), stop=(kc == 7))
                    return ins
                P.op("pe", mm, reads=[("WB", s)] + [("U", kc, tt // 4) for kc in range(8)], writes=[("ps", b)])
                P.op("dve", lambda e, b=b, tt=tt: e.tensor_copy(
                    out=V[:, tt, 0:nh * (hd + 1)].rearrange("p (h e) -> p h e", e=hd + 1)[:, :, 0:hd],
                    in_=bank(b)[:, 0:256].rearrange("p (h d) -> p h d", d=hd)),
                    writes=[("ps", b), ("V", tt)])

        def transpose_out(ys, nblk, hbase, tok0, reads):
            tp = nxt("tp", 2)
            toff = tp * 512

            def tr(e, ys=ys, toff=toff):
                ins = None
                for i in range(nblk):
                    ins = e.transpose(out=PST[:, toff + i * 128:toff + (i + 1) * 128],
                                      in_=YS[ys][:, i * 128:(i + 1) * 128], identity=IDb[:])
                return ins
            P.op("pe", tr, reads=["ID", ("YS", ys)], writes=["pst"])
            P.op("dve", lambda e, toff=toff: e.tensor_copy(
                out=H[:, hbase:hbase + nblk, tok0:tok0 + 128],
                in_=PST[:, toff:toff + nblk * 128].rearrange("p (c q) -> p c q", q=128)),
                writes=["pst"] + [("H", hbase + i, tok0 // 512) for i in range(nblk)])

        LA = int(os.environ.get('KLA', '2'))

        def run_pipelined(steps):
            n = len(steps)
            LAG2, LAG3 = int(os.environ.get('KLAG2', '4')), int(os.environ.get('KLAG3', '8'))
            for idx in range(n + LA + LAG3):
                if idx < n:
                    stp = steps[idx]
                    if stp.get("pre"):
                        stp["pre"]()
                    stp["qk"]()
                k = idx - LA
                if 0 <= k < n:
                    stp = steps[k]
                    stp["pv"]()
                    if stp.get("post"):
                        stp["post"]()
                k2 = idx - LA - LAG2
                if 0 <= k2 < n and steps[k2].get("post2"):
                    steps[k2]["post2"]()
                k2b = idx - LA - LAG2 - 1
                if 0 <= k2b < n and steps[k2b].get("post2b"):
                    steps[k2b]["post2b"]()
                k3 = idx - LA - LAG3
                if 0 <= k3 < n and steps[k3].get("post3"):
                    steps[k3]["post3"]()

        def attn_A(g, qkv_base):
            P.dma("pool", lambda e, sem: e.dma_start(out=BI[:, 0:2048], in_=ba_d[g]).then_inc(sem, 16), "bi",
                  writes=["BI"])
            s = wneed(qkv_base + 0)
            proj_fm(s, 0, 8, 0); proj_fm(s, 128, 9, 1)
            s = wneed(qkv_base + 1)
            proj_fm(s, 0, 10, 0); proj_fm(s, 128, 11, 1)
            s = wneed(qkv_base + 2)
            proj_v(s, 4, 64)
            jtype = [0, 1, 1, 2, 3]
            steps = []
            for pr in range(16):
                qz = nxt("qz", 2)
                q0 = pr * 128
                bo = pr % 2
                js = [j for j in range(5) if pr - 4 + j >= 0]
                for j in js:
                    kt = pr - 4 + j
                    bs = 2 + nxt("psa", 5)
                    ty = jtype[j]
                    ei = nxt("e", NE)
                    stp = {}
                    if j == js[0]:
                        def pre(qz=qz, q0=q0, pr=pr):
                            for half in range(2):
                                psl = slice(half * 64, half * 64 + 64)
                                P.op("dve", lambda e, psl=psl, half=half: e.tensor_scalar(
                                    out=QZ[qz][psl, 0:512].rearrange("p (c h q) -> p c h q", c=2, h=2)[:, :, half, :],
                                    in0=H[psl, 8:10, q0:q0 + 128], scalar1=0.125, scalar2=None, op0=ALU.mult),
                                    reads=[("H", 8, pr // 4), ("H", 9, pr // 4)], writes=[("QZ", qz, half)])
                        stp["pre"] = pre

                    def qk(bs=bs, ty=ty, kt=kt, qz=qz, ei=ei):
                        def mm(e):
                            e.matmul(bank(bs), lhsT=IDb[:], rhs=BI[:, ty * 512:(ty + 1) * 512], start=True, stop=False)
                            ins = None
                            for hl in range(4):
                                ins = e.matmul(bank(bs)[:, hl * 128:(hl + 1) * 128],
                                               lhsT=H[:, 10 + hl // 2, kt * 128:(kt + 1) * 128],
                                               rhs=QZ[qz][:, hl * 128:(hl + 1) * 128], start=False, stop=(hl == 3))
                            return ins
                        P.op("pe", mm, reads=["ID", "BI", ("QZ", qz, 0), ("QZ", qz, 1), ("H", 10, kt // 4),
                                              ("H", 11, kt // 4)], writes=[("ps", bs)])
                        P.op("act", lambda e: e.activation(out=E[ei][:], in_=bank(bs), func=AF.Exp),
                             writes=[("ps", bs), ("E", ei)])
                    stp["qk"] = qk

                    def pv(ei=ei, kt=kt, bo=bo, first=(j == js[0]), last=(j == js[-1])):
                        def mm(e):
                            ins = None
                            for hl in range(4):
                                ins = e.matmul(bank(bo)[:, hl * 65:(hl + 1) * 65], lhsT=E[ei][:, hl * 128:(hl + 1) * 128],
                                               rhs=V[:, kt, hl * 65:(hl + 1) * 65], start=(first and hl == 0),
                                               stop=last, skip_group_check=True)
                            return ins
                        P.op("pe", mm, reads=[("E", ei), ("V", kt)], writes=[("ps", bo)])
                    stp["pv"] = pv
                    if j == js[-1]:
                        sm = nxt("st", NST)
                        ys = nxt("ys", NYS)

                        def post(bo=bo, sm=sm, ys=ys):
                            P.op("dve", lambda e: e.reciprocal(
                                out=SM[sm][:, 0:4],
                                in_=bank(bo)[:, 0:260].rearrange("p (h e) -> p h e", e=65)[:, :, 64]),
                                writes=[("ps", bo), ("SM", sm)])
                            P.op("dve", lambda e: e.tensor_tensor(
                                out=YS[ys][:, 0:256].rearrange("p (h d) -> p h d", d=64),
                                in0=bank(bo)[:, 0:260].rearrange("p (h e) -> p h e", e=65)[:, :, 0:64],
                                in1=SM[sm][:, 0:4].unsqueeze(2).to_broadcast([128, 4, 64]), op=ALU.mult),
                                reads=[("SM", sm)], writes=[("ps", bo), ("YS", ys)])
                        stp["post"] = post

                        def post2(ys=ys, q0=q0):
                            transpose_out(ys, 2, 2 * g, q0, None)
                        stp["post2"] = post2
                    steps.append(stp)
            run_pipelined(steps)

        def attn_B(g, qkv_base):
            P.dma("pool", lambda e, sem: e.dma_start(out=BI[:, 0:1280], in_=bb_d[g]).then_inc(sem, 16), "bi",
                  writes=["BI"])
            for hh_ in range(2):
                P.op("dve", lambda e, hh_=hh_: e.tensor_scalar(
                    out=BI[:, hh_ * 640:(hh_ + 1) * 640], in0=BI[:, hh_ * 640:(hh_ + 1) * 640],
                    scalar1=PR[:, PC_T5C + 2 * g + hh_:PC_T5C + 2 * g + hh_ + 1], scalar2=None, op0=ALU.subtract),
                    reads=["BI", "PR"], writes=["BI"])
            s = wneed(qkv_base + 0)
            proj_fm(s, 0, 8, 0); proj_fm(s, 128, 9, 1)
            s = wneed(qkv_base + 1)
            proj_fm(s, 0, 10, 0); proj_fm(s, 128, 11, 1)
            s = wneed(qkv_base + 2)
            proj_v(s, 2, 128)
            steps = []
            for hh in range(2):
                h = 2 * g + hh
                for QB in range(8):
                    qz = nxt("qz", 2)
                    accb = 2 * nxt("accb", 2)
                    t0 = QB * 256
                    for j in range(2 * QB + 2):
                        i0 = max(j, 2 * QB)
                        nq = 2 * QB + 2 - i0
                        N = nq * 128
                        qoff = (i0 - 2 * QB) * 128
                        near = (i0 - j) <= 1
                        if True:
                            bs = 4 + nxt("psb", 3)
                            ei = nxt("e", NE)
                            stp = {}
                            if j == 0:
                                def pre(qz=qz, t0=t0, hh=hh, QB=QB):
                                    for rr in range(2):
                                        psl = slice(rr * 64, rr * 64 + 64)
                                        P.op("dve", lambda e, psl=psl, rr=rr: e.tensor_scalar(
                                            out=QZ[qz][psl, rr * 256:(rr + 1) * 256], in0=H[psl, 8 + hh, t0:t0 + 256],
                                            scalar1=0.125, scalar2=None, op0=ALU.mult),
                                            reads=[("H", 8 + hh, QB // 2)], writes=[("QZ", qz, rr)])
                                stp["pre"] = pre

                            def qk(bs=bs, ei=ei, near=near, N=N, i0=i0, j=j, qz=qz, qoff=qoff, hh=hh, h=h):
                                def mm(e):
                                    ins = None
                                    for r in range(2):
                                        o_ = bank(bs)[:, r * 256:r * 256 + N]
                                        if near:
                                            boff = hh * 640 + (i0 - j) * 128
                                            e.matmul(o_, lhsT=IDb[:], rhs=BI[:, boff:boff + N],
                                                     start=(r == 0), stop=False, skip_group_check=True)
                                        ins = e.matmul(o_, lhsT=H[:, 10 + hh, j * 128:(j + 1) * 128],
                                                       rhs=QZ[qz][:, r * 256 + qoff:r * 256 + qoff + N],
                                                       start=(r == 0 and not near), stop=True, skip_group_check=True)
                                    return ins
                                P.op("pe", mm, reads=["ID", "BI", ("QZ", qz, 0), ("QZ", qz, 1), ("H", 10 + hh, j // 4)],
                                     writes=[("ps", bs)])
                                src_ = bank(bs).rearrange("p (r q) -> p r q", r=2)[:, :, 0:N]
                                dst_ = E[ei][:].rearrange("p (r q) -> p r q", r=2)[:, :, 0:N]
                                P.op("act", lambda e: e.activation(out=dst_, in_=src_, func=AF.Exp),
                                     writes=[("ps", bs), ("E", ei)])
                            stp["qk"] = qk

                            def pv(ei=ei, nq=nq, i0=i0, j=j, QB=QB, hh=hh, accb=accb):
                                def mm(e):
                                    ins = None
                                    for ii in range(nq):
                                        i = i0 + ii
                                        bi_ = accb + i - 2 * QB
                                        for r in range(2):
                                            ins = e.matmul(bank(bi_)[:, r * 129:(r + 1) * 129],
                                                           lhsT=E[ei][:, r * 256 + ii * 128:r * 256 + (ii + 1) * 128],
                                                           rhs=V[:, j, hh * 129:(hh + 1) * 129],
                                                           start=(j == 0 and r == 0), stop=(j == i and r == 1),
                                                           skip_group_check=True)
                                    return ins
                                P.op("pe", mm, reads=[("E", ei), ("V", j)],
                                     writes=[("ps", accb + i0 + ii - 2 * QB) for ii in range(nq)])
                            stp["pv"] = pv
                            r = 1
                            if j >= 2 * QB and r == 1:
                                sg = nxt("st", NST)
                                ys = nxt("ys", NYS)

                                def post(i=j, QB=QB, sg=sg, accb=accb):
                                    bi_ = accb + i - 2 * QB
                                    P.op("dve", lambda e: e.tensor_copy(out=STG[sg][:], in_=bank(bi_)[:, 0:258]),
                                         writes=[("ps", bi_), ("STG", sg)])
                                    P.op("dve", lambda e: e.reciprocal(
                                        out=SM[sg][:, 0:2],
                                        in_=STG[sg][:, 0:258].rearrange("p (r e) -> p r e", e=129)[:, :, 128]),
                                        reads=[("STG", sg)], writes=[("SM", sg)])
                                    P.op("dve", lambda e: e.tensor_scalar(
                                        out=STG[sg][:, 129:257], in0=STG[sg][:, 129:257], scalar1=SM[sg][:, 1:2],
                                        scalar2=LAMT[:, 2:3], op0=ALU.mult, op1=ALU.mult),
                                        reads=[("STG", sg), ("SM", sg), "LAM"], writes=[("STG", sg)])
                                    P.op("dve", lambda e: e.scalar_tensor_tensor(
                                        out=STG[sg][:, 0:128], in0=STG[sg][:, 0:128], scalar=SM[sg][:, 0:1],
                                        in1=STG[sg][:, 129:257], op0=ALU.mult, op1=ALU.add),
                                        reads=[("STG", sg), ("SM", sg)], writes=[("STG", sg)])
                                    P.op("dve", lambda e: e.scalar_tensor_tensor(
                                        out=STG[sg][:, 129:257], in0=STG[sg][:, 0:128], scalar=1.0,
                                        in1=STG[sg][:, 0:128], op0=ALU.mult, op1=ALU.mult, accum_out=SM[sg][:, 2:3]),
                                        reads=[("STG", sg)], writes=[("STG", sg), ("SM2", sg)])
                                stp["post"] = post

                                def post2(i=j, h=h, sg=sg, ys=ys):
                                    P.op("act", lambda e: e.activation(
                                        out=SM[sg][:, 3:4], in_=SM[sg][:, 2:3], func=AF.Ln, bias=EPST[:, 0:1],
                                        scale=1.0 / 128),
                                        reads=[("SM2", sg), "EPS"], writes=[("SM3", sg)])
                                stp["post2"] = post2

                                def post2b(i=j, h=h, sg=sg, ys=ys):
                                    P.op("act", lambda e: e.activation(
                                        out=SM[sg][:, 4:5], in_=SM[sg][:, 3:4], func=AF.Exp, scale=-0.5),
                                        reads=[("SM3", sg)], writes=[("SM4", sg)])
                                    P.op("dve", lambda e: e.scalar_tensor_tensor(
                                        out=YS[ys][:, 0:128], in0=STG[sg][:, 0:128], scalar=SM[sg][:, 4:5], in1=G08[:],
                                        op0=ALU.mult, op1=ALU.mult),
                                        reads=[("STG", sg), ("SM4", sg), "G08"], writes=[("YS", ys)])
                                stp["post2b"] = post2b

                                def post3(i=j, h=h, ys=ys):
                                    transpose_out(ys, 1, 4 + h, i * 128, None)
                                stp["post3"] = post3
                            steps.append(stp)
            run_pipelined(steps)

        WOC = [0]

        def mix_stage():
            base = plan_pos["qkv"]
            rms_norm(PC_GM, pre_banks=(STATB if fuse1 else None))
            P.op("dve", lambda e: e.scalar_tensor_tensor(
                out=LTMP[:], in0=PR[:, PC_LQ1:PC_LQ1 + 64], scalar=1.0, in1=PR[:, PC_LK1:PC_LK1 + 64],
                op0=ALU.mult, op1=ALU.mult, accum_out=LAMT[:, 0:1]), reads=["PR"], writes=["LTMP", "LAM0"])
            P.op("dve", lambda e: e.scalar_tensor_tensor(
                out=LTMP[:], in0=PR[:, PC_LQ2:PC_LQ2 + 64], scalar=1.0, in1=PR[:, PC_LK2:PC_LK2 + 64],
                op0=ALU.mult, op1=ALU.mult, accum_out=LAMT[:, 1:2]), reads=["PR"], writes=["LTMP", "LAM1"])
            P.op("act", lambda e: e.activation(out=LAMT[:, 4:6], in_=LAMT[:, 0:2], func=AF.Exp),
                 reads=["LAM0", "LAM1"], writes=["LAME"])
            P.op("dve", lambda e: e.tensor_tensor(out=LAMT[:, 3:4], in0=LAMT[:, 5:6], in1=LAMT[:, 4:5],
                                                  op=ALU.subtract), reads=["LAME"], writes=["LAMD"])
            P.op("dve", lambda e: e.tensor_scalar(out=LAMT[:, 2:3], in0=LAMT[:, 3:4], scalar1=-LAMBDA_INIT,
                                                  scalar2=None, op0=ALU.add), reads=["LAMD"], writes=["LAM"])
            P.op("dve", lambda e: e.tensor_scalar(out=G08[:], in0=PR[:, PC_SUBG:PC_SUBG + 128],
                                                  scalar1=1.0 - LAMBDA_INIT, scalar2=None, op0=ALU.mult),
                 reads=["PR"], writes=["G08"])
            if MIXDBG == -1:
                pass
            P.op("dve", lambda e: e.memset(V[:], 1.0), writes=[("V", tt) for tt in range(16)])
            for q in range(2):
                P.op("dve", lambda e, q=q: e.memset(QZ[q][:], 0.0), writes=[("QZ", q, 0), ("QZ", q, 1)])
            if 0 <= MIXDBG < 2:
                return
            for g in range(2 if MIXDBG >= 0 else 0):
                attn_A(g, base + 3 * g)
                if MIXDBG < 5:
                    return
            P.op("dve", lambda e: e.memset(V[:], 1.0), writes=[("V", tt) for tt in range(16)])
            for q in range(2):
                P.op("dve", lambda e, q=q: e.memset(QZ[q][:], 0.0), writes=[("QZ", q, 0), ("QZ", q, 1)])
            if 0 <= MIXDBG < 6:
                return
            for g in range(2 if MIXDBG >= 0 else 0):
                attn_B(g, base + 6 + 3 * g)
                if MIXDBG < 9:
                    return
            if 0 <= MIXDBG < 10:
                return
            mb = plan_pos["merge"]
            for mg in range(2):
                gb = mb + mg * 10
                for mm_ in range(4 if MIXDBG >= 0 else 0):
                    m = mg * 4 + mm_
                    sgw = wneed(gb + 2 * mm_)
                    sbr = wneed(gb + 2 * mm_ + 1)
                    for tb in range(4):
                        tsl = slice(tb * 512, (tb + 1) * 512)
                        bc, bd, ba, bb_ = nxt("ps", 7), nxt("ps", 7), nxt("ps", 7), nxt("ps", 7)

                        def mmg(e, sgw=sgw, tsl=tsl, bc=bc, bd=bd):
                            ins = None
                            for kc in range(8):
                                e.matmul(bank(bc), lhsT=WB[sgw][:, kc * 256:kc * 256 + 128], rhs=U[:, kc, tsl],
                                         start=(kc == 0), stop=(kc == 7))
                            for kc in range(8):
                                ins = e.matmul(bank(bd), lhsT=WB[sgw][:, kc * 256 + 128:kc * 256 + 256],
                                               rhs=U[:, kc, tsl], start=(kc == 0), stop=(kc == 7))
                            return ins
                        P.op("pe", mmg, reads=[("WB", sgw)] + [("U", kc, tb) for kc in range(8)],
                             writes=[("ps", bc), ("ps", bd)])

                        def mmp(e, sbr=sbr, tsl=tsl, ba=ba, bb_=bb_):
                            for kc in range(4):
                                e.matmul(bank(ba), lhsT=WB[sbr][:, kc * 128:(kc + 1) * 128], rhs=H[:, kc, tsl],
                                         start=(kc == 0), stop=(kc == 3))
                            ins = None
                            for kc in range(4):
                                ins = e.matmul(bank(bb_), lhsT=WB[sbr][:, 512 + kc * 128:512 + (kc + 1) * 128],
                                               rhs=H[:, 4 + kc, tsl], start=(kc == 0), stop=(kc == 3))
                            return ins
                        P.op("pe", mmp, reads=[("WB", sbr)] + [("H", kc, tb) for kc in range(8)],
                             writes=[("ps", ba), ("ps", bb_)])
                        s1, s2 = nxt("sc", NSC), nxt("sc", NSC)
                        P.op("act", lambda e, bc=bc, s1=s1, m=m: e.activation(
                            out=SC[s1][:], in_=bank(bc), func=AF.Sigmoid, bias=PR[:, PC_BG + m:PC_BG + m + 1]),
                            reads=["PR"], writes=[("ps", bc), ("SC", s1)])
                        P.op("act", lambda e, bd=bd, s2=s2, m=m: e.activation(
                            out=SC[s2][:], in_=bank(bd), func=AF.Sigmoid, bias=PR[:, PC_BG + 8 + m:PC_BG + 9 + m]),
                            reads=["PR"], writes=[("ps", bd), ("SC", s2)])
                        P.op("dve", lambda e, ba=ba, s1=s1: e.tensor_tensor(
                            out=SC[s1][:], in0=bank(ba), in1=SC[s1][:], op=ALU.mult),
                            reads=[("SC", s1)], writes=[("ps", ba), ("SC", s1)])
                        P.op("dve", lambda e, bb_=bb_, s2=s2: e.tensor_tensor(
                            out=SC[s2][:], in0=bank(bb_), in1=SC[s2][:], op=ALU.mult),
                            reads=[("SC", s2)], writes=[("ps", bb_), ("SC", s2)])
                        P.op("dve", lambda e, s1=s1, s2=s2, mm_=mm_, tsl=tsl: e.tensor_tensor(
                            out=H[:, 8 + mm_, tsl], in0=SC[s1][:], in1=SC[s2][:], op=ALU.add),
                            reads=[("SC", s1), ("SC", s2)], writes=[("H", 8 + mm_, tb)])
                if MIXDBG < 12 and MIXDBG >= 0:
                    return
                sos = [wneed(gb + 8), wneed(gb + 9)]
                for tb in range(4):
                    for half in range(2):
                        so = sos[half]
                        for m4 in range(4):
                            mo = half * 4 + m4
                            tsl = slice(tb * 512, (tb + 1) * 512)
                            b = nxt("ps", 7)

                            def mm(e, so=so, m4=m4, tsl=tsl, b=b):
                                ins = None
                                for kc in range(4):
                                    ins = e.matmul(bank(b), lhsT=WB[so][:, m4 * 512 + kc * 128:m4 * 512 + (kc + 1) * 128],
                                                   rhs=H[:, 8 + kc, tsl], start=(kc == 0), stop=(kc == 3))
                                return ins
                            P.op("pe", mm, reads=[("WB", so)] + [("H", 8 + kc, tb) for kc in range(4)],
                                 writes=[("ps", b)])
                            P.op("dve", lambda e, b=b, mo=mo, tsl=tsl: e.scalar_tensor_tensor(
                                out=X[:, mo, tsl], in0=bank(b), scalar=1.0, in1=X[:, mo, tsl],
                                op0=ALU.mult, op1=ALU.add),
                                writes=[("ps", b), ("X", mo, tb)])
                if MIXDBG < 14:
                    return

        fuse1 = ("ffn1" in stages) and ("mix" in stages) and FFNREP == 0
        if "ffn1" in stages:
            ffn_stage("ffn1", PC_G1, fuse_next=fuse1)
            for rep in range(FFNREP):
                ffn_stage("ffn1r%d" % rep, PC_G1)
        if "mix" in stages:
            mix_stage()
        fuse2 = "ffn2" in stages
        if "ffn2" in stages:
            ffn_stage("ffn2", PC_G2, fuse_next=True)
        rms_norm(PC_GF, final=True, pre_banks=(STATB if fuse2 else None))
        P.op("sp", lambda e: None, reads=[("OUT", tb) for tb in range(4)])
        P.emit_all(nc, st)
    return nc


def _t5_bucket_np(rel):
    nb = 16
    ret = np.where(rel > 0, nb, 0)
    n = np.abs(rel)
    max_exact = nb // 2
    nf = np.maximum(n, 1).astype(np.float32)
    large = max_exact + (np.log(nf / np.float32(max_exact)) / np.float32(math.log(128 / max_exact))
                         * np.float32(nb - max_exact)).astype(np.int32)
    large = np.minimum(large, nb - 1)
    return ret + np.where(n < max_exact, n, large)


def _kc_tile(w_cols):
    n = w_cols.shape[1]
    return np.ascontiguousarray(w_cols.reshape(8, 128, n).transpose(1, 0, 2)).reshape(128, 8 * n)


def _prep_shared(inp):
    f = lambda a: np.asarray(a, dtype=np.float32)
    sh = {}

    def ffn_w(w_in, w_out):
        w_in = f(w_in)[0]
        w_out = f(w_out)[0]
        wi = np.empty((NCH, 128, 2048), np.float32)
        for c in range(NCH):
            cols = np.concatenate([w_in[:, c * 128:(c + 1) * 128], w_in[:, DFF + c * 128:DFF + (c + 1) * 128]], axis=1)
            wi[c] = _kc_tile(cols)
        wo = w_out.reshape(2, NHP, 128, 8, 128).transpose(0, 3, 2, 1, 4)
        wo = np.ascontiguousarray(wo).reshape(2, 8, 128, NHP * 128)
        return wi, wo

    sh["w1i"], sh["w1o"] = ffn_w(inp["ffn1_w_in"], inp["ffn1_w_out"])
    sh["w2i"], sh["w2o"] = ffn_w(inp["ffn2_w_in"], inp["ffn2_w_out"])
    wm = f(inp["w_mix_in"])[0]
    qkv = []
    for (qo, ko, vo) in ((0, 512, 1024), (1536, 2048, 2560)):
        for g in range(2):
            for off in (qo, ko, vo):
                qkv.append(_kc_tile(wm[:, off + g * 256:off + (g + 1) * 256]))
    sh["wqkv"] = np.stack(qkv)
    sh["wg"] = np.stack([_kc_tile(np.concatenate([wm[:, 3072 + m * 128:3072 + (m + 1) * 128],
                                                  wm[:, 4096 + m * 128:4096 + (m + 1) * 128]], axis=1))
                         for m in range(8)])
    wa = f(inp["w_branch_a"])[0]
    wb = f(inp["w_branch_b"])[0]

    def br_tile(w, m):
        return np.ascontiguousarray(w[:, m * 128:(m + 1) * 128].reshape(4, 128, 128).transpose(1, 0, 2)).reshape(128, 512)
    sh["wbr"] = np.stack([np.concatenate([br_tile(wa, m), br_tile(wb, m)], axis=1) for m in range(8)])
    wo = f(inp["w_o"])[0]
    wo_t = np.empty((2, 2, 128, 2048), np.float32)
    for mg in range(2):
        for half in range(2):
            blk = wo[mg * 512:(mg + 1) * 512, half * 512:(half + 1) * 512]
            t = blk.reshape(4, 128, 4, 128).transpose(1, 2, 0, 3)
            wo_t[mg, half] = np.ascontiguousarray(t).reshape(128, 2048)
    sh["wo"] = wo_t

    par = np.zeros((128, NPAR), np.float32)
    for name, col in (("ffn1_norm", PC_G1), ("mix_norm", PC_GM), ("ffn2_norm", PC_G2)):
        par[:, col:col + 8] = f(inp[name])[0].reshape(8, 128).T
    par[:, PC_GF:PC_GF + 8] = f(inp["final_norm"]).reshape(8, 128).T
    par[:, PC_BG:PC_BG + 16] = f(inp["b_gate"])[0].reshape(16, 128).T
    for name, col in (("lambda_q1", PC_LQ1), ("lambda_k1", PC_LK1), ("lambda_q2", PC_LQ2), ("lambda_k2", PC_LK2)):
        par[:, col:col + 64] = f(inp[name])[0][None, :]
    par[:, PC_SUBG:PC_SUBG + 128] = f(inp["subln_g"])[0][None, :]
    t5 = f(inp["t5_bias"])
    par[:, PC_T5C:PC_T5C + 4] = t5[:, 15][None, :]
    sh["params"] = par
    sh["ident"] = np.eye(128, dtype=np.float32)

    rel_tab = f(inp["rel_bias_a"])[0]
    kk = np.arange(128)[:, None]
    qq = np.arange(128)[None, :]
    idx3 = np.clip(-128 + kk - qq, -128, 128) + 128
    idx4 = np.clip(kk - qq, -128, 128) + 128
    mask0 = (kk < 64) & (qq >= 64)
    mask4 = (kk >= 64) & (qq < 64)
    bA = np.empty((2, 128, 4, 4, 128), np.float32)
    for h in range(8):
        g, hl = divmod(h, 4)
        c0 = np.broadcast_to(rel_tab[h, 0], (128, 128))
        bA[g, :, 0, hl, :] = np.where(mask0, np.float32(MASKV), c0)
        bA[g, :, 1, hl, :] = c0
        bA[g, :, 2, hl, :] = rel_tab[h][idx3]
        bA[g, :, 3, hl, :] = np.where(mask4, np.float32(MASKV), rel_tab[h][idx4])
    sh["biasA"] = bA.reshape(2, 128, 2048)
    bD = _t5_bucket_np(kk - qq)
    bS = _t5_bucket_np(kk - qq - 128)
    bB = np.empty((2, 128, 2, 5, 128), np.float32)
    for h in range(4):
        g, hh = divmod(h, 2)
        bB[g, :, hh, 0, :] = np.where(mask4, np.float32(MASKV), t5[h][bD])
        bB[g, :, hh, 1, :] = t5[h][bS]
        bB[g, :, hh, 2:5, :] = t5[h, 15]
    sh["biasB"] = bB.reshape(2, 128, 1280)
    return sh


_NC_CACHE = {}


def _run(inp, stages=("ffn1", "mix", "ffn2"), ncores=8):
    x = np.asarray(inp["x"], dtype=np.float32)
    sh = _prep_shared(inp)
    key = tuple(stages)
    if key not in _NC_CACHE:
        _NC_CACHE[key] = build_program(stages)
    nc = _NC_CACHE[key]
    in_maps = []
    for b in range(ncores):
        xT = np.ascontiguousarray(x[b].T.reshape(8, 128, S).transpose(1, 0, 2))
        m = {"xT": xT}
        m.update(sh)
        in_maps.append(m)
    res = run_bass_kernel_spmd(nc, in_maps, core_ids=list(range(ncores)))
    outs = []
    for b in range(ncores):
        oT = np.asarray(res.results[b]["outT"])
        outs.append(oT.transpose(2, 1, 0).reshape(S, D))
    return np.stack(outs).astype(np.float32)


def kernel(**inputs):
    return _run(inputs)
```

```python
import contextlib
import math
import os

MIXDBG = int(os.environ.get('MIXDBG', '99'))
STRICT = os.environ.get('KSTRICT', '1') == '1'

import numpy as np

import concourse.bass as bass
import concourse.mybir as mybir
from concourse.bass_utils import run_bass_kernel_spmd

F32 = mybir.dt.float32
BF16 = mybir.dt.bfloat16
AF = mybir.ActivationFunctionType
ALU = mybir.AluOpType

D = 1024
S = 2048
DFF = 2816
NCH = 22
NHP = 11
EPS = 1e-6
MASKV = -30000.0
LAMBDA_INIT = 0.8 - 0.6 * math.exp(0.0)

PC_G1, PC_GM, PC_G2, PC_GF, PC_BG = 0, 8, 16, 24, 32
PC_LQ1, PC_LK1, PC_LQ2, PC_LK2 = 48, 112, 176, 240
PC_SUBG = 304
PC_T5C = 432
NPAR = 436

ENG_NAMES = ("pe", "act", "dve", "pool", "sp")


class _Op:
    __slots__ = ("eng", "emit", "deps", "is_dma", "sig", "need_sig")

    def __init__(self, eng, emit, is_dma):
        self.eng = eng
        self.emit = emit
        self.is_dma = is_dma
        self.deps = []
        self.sig = None
        self.need_sig = False


class Prog:
    def __init__(self):
        self.ops = {e: [] for e in ENG_NAMES}
        self.last_writer = {}
        self.readers = {}
        self.dma_counts = {}

    def _add(self, o, reads, writes):
        deps = {}
        for k in reads:
            lw = self.last_writer.get(k)
            if lw is not None:
                deps[id(lw)] = (lw, True)
        for k in writes:
            lw = self.last_writer.get(k)
            if lw is not None and id(lw) not in deps:
                deps[id(lw)] = (lw, False)
            for r in self.readers.get(k, ()):
                if id(r) not in deps:
                    deps[id(r)] = (r, False)
        for k in reads:
            self.readers.setdefault(k, []).append(o)
        for k in writes:
            self.last_writer[k] = o
            self.readers[k] = []
        for d, raw in deps.values():
            if d is o:
                continue
            if d.eng == o.eng and not d.is_dma and not o.is_dma and not raw and not STRICT:
                continue
            d.need_sig = True
            o.deps.append(d)
        self.ops[o.eng].append(o)
        return o

    def op(self, eng, emit, reads=(), writes=()):
        return self._add(_Op(eng, emit, False), reads, writes)

    def dma(self, eng, emit, semkey, n_dmas=1, reads=(), writes=()):
        o = _Op(eng, emit, True)
        c = self.dma_counts.get(semkey, 0) + 16 * n_dmas
        self.dma_counts[semkey] = c
        o.sig = (("dma", semkey), c)
        return self._add(o, reads, writes)

    def emit_all(self, nc, stack):
        sems = {}
        for e in ENG_NAMES:
            sems[("eng", e)] = stack.enter_context(nc.semaphore("s_" + e))
        for k in self.dma_counts:
            sems[("dma", k)] = stack.enter_context(nc.semaphore("d_" + str(k)))
        for e in ENG_NAMES:
            t = 0
            for o in self.ops[e]:
                if o.is_dma:
                    continue
                if o.need_sig:
                    t += 1
                    o.sig = (("eng", e), t)
        block = stack.enter_context(nc.Block())
        prog = self

        def run(e):
            def body(eng):
                waited = {}
                for o in prog.ops[e]:
                    need = {}
                    for d in o.deps:
                        sk, v = d.sig
                        if waited.get(sk, 0) >= v:
                            continue
                        if need.get(sk, 0) < v:
                            need[sk] = v
                    for sk, v in need.items():
                        eng.wait_ge(sems[sk], v)
                        waited[sk] = v
                    if o.is_dma:
                        o.emit(eng, sems[o.sig[0]])
                    else:
                        ins = o.emit(eng)
                        if o.need_sig:
                            ins.then_inc(sems[o.sig[0]], 1)
            return body

        block.tensor(run("pe"))
        block.scalar(run("act"))
        block.vector(run("dve"))
        block.gpsimd(run("pool"))
        block.sync(run("sp"))


def build_program(stages=("ffn1", "mix", "ffn2")):
    nc = bass.Bass("TRN2", target_bir_lowering=False)
    dt_in = lambda name, shape: nc.dram_tensor(name, list(shape), F32, kind="ExternalInput").ap()
    xT_d = dt_in("xT", [128, 8, S])
    par_d = dt_in("params", [128, NPAR])
    ident_d = dt_in("ident", [128, 128])
    w1i_d = dt_in("w1i", [NCH, 128, 2048])
    w1o_d = dt_in("w1o", [2, 8, 128, NHP * 128])
    w2i_d = dt_in("w2i", [NCH, 128, 2048])
    w2o_d = dt_in("w2o", [2, 8, 128, NHP * 128])
    wqkv_d = dt_in("wqkv", [12, 128, 2048])
    wg_d = dt_in("wg", [8, 128, 2048])
    wbr_d = dt_in("wbr", [8, 128, 1024])
    wo_d = dt_in("wo", [2, 2, 128, 2048])
    ba_d = dt_in("biasA", [2, 128, 2048])
    bb_d = dt_in("biasB", [2, 128, 1280])
    out_d = nc.dram_tensor("outT", [128, 8, S], F32, kind="ExternalOutput").ap()

    with contextlib.ExitStack() as st:
        sb = lambda name, shape, dt: st.enter_context(nc.sbuf_tensor(name, list(shape), dt))
        X = sb("X", [128, 8, S], F32)
        U = sb("U", [128, 8, S], BF16)
        H = sb("H", [128, 12, S], BF16)
        V = sb("V", [128, 16, 264], BF16)
        RB = sb("RB", [128, S], F32)
        NWB = 4
        WB = [sb(f"WB{i}", [128, 2048], BF16) for i in range(NWB)]
        BI = sb("BI", [128, 2048], BF16)
        NE = 4
        E = [sb(f"E{i}", [128, 512], BF16) for i in range(NE)]
        NSC = 4
        SC = [sb(f"SC{i}", [128, 512], F32) for i in range(NSC)]
        QZ = [sb(f"QZ{i}", [128, 1024], BF16) for i in range(2)]
        NYS = 5
        YS = [sb(f"YS{i}", [128, 256], BF16) for i in range(NYS)]
        NST = 5
        STG = [sb(f"STG{i}", [128, 258], F32) for i in range(NST)]
        SM = [sb(f"SM{i}", [128, 8], F32) for i in range(NST)]
        PR = sb("PR", [128, NPAR], F32)
        IDb = sb("IDb", [128, 128], BF16)
        ONESb = sb("ONESb", [128, 128], BF16)
        EPST = sb("EPST", [128, 1], F32)
        LAMT = sb("LAMT", [128, 8], F32)
        G08 = sb("G08", [128, 128], F32)
        LTMP = sb("LTMP", [128, 64], F32)
        PS = st.enter_context(nc.psum_tensor("PS", [128, 8 * 512], F32))
        PST = PS[:, 7 * 512:8 * 512].bitcast(BF16)

        P = Prog()
        bank = lambda b: PS[:, b * 512:(b + 1) * 512]
        rot = {"ps": 0, "psa": 0, "psb": 0, "accb": 0, "e": 0, "sc": 0, "ys": 0, "st": 0, "qz": 0, "tp": 0}

        def nxt(name, n):
            v = rot[name]
            rot[name] = (v + 1) % n
            return v

        wplan = []
        wstate = {"emitted": 0}

        def wslot(i):
            return i % NWB

        def wneed(i, ahead=int(os.environ.get('WAHEAD', '2'))):
            hi = min(len(wplan) - 1, i + ahead)
            while wstate["emitted"] <= hi:
                L = wstate["emitted"]
                src, ncols = wplan[L]
                s = wslot(L)
                P.dma("pool", lambda e, sem, src=src, s=s, ncols=ncols:
                      e.dma_start(out=WB[s][:, 0:ncols], in_=src).then_inc(sem, 16),
                      f"wb{s}", writes=[("WB", s)])
                wstate["emitted"] += 1
            return wslot(i)

        P.dma("sp", lambda e, sem: e.dma_start(out=PR[:], in_=par_d).then_inc(sem, 16), "pr", writes=["PR"])
        for tb in range(4):
            P.dma("sp", lambda e, sem, tb=tb: e.dma_start(
                out=X[:, :, tb * 512:(tb + 1) * 512], in_=xT_d[:, :, tb * 512:(tb + 1) * 512]).then_inc(sem, 16),
                f"x{tb}", writes=[("X", c, tb) for c in range(8)])
        P.dma("pool", lambda e, sem: e.dma_start(out=IDb[:], in_=ident_d).then_inc(sem, 16), "id", writes=["ID"])
        P.op("dve", lambda e: e.memset(ONESb[:], 1.0), writes=["ONES"])
        P.op("dve", lambda e: e.memset(EPST[:], EPS), writes=["EPS"])

        def rms_norm(gcol, final=False):
            banks = [nxt("ps", 7) for _ in range(4)]

            def stats(tb):
                tsl = slice(tb * 512, (tb + 1) * 512)
                for c in range(8):
                    P.op("act", lambda e, c=c: e.activation(out=H[:, c, tsl], in_=X[:, c, tsl], func=AF.Square),
                         reads=[("X", c, tb)], writes=[("H", c, tb)])

                def mm(e):
                    ins = None
                    for c in range(8):
                        ins = e.matmul(bank(banks[tb]), lhsT=ONESb[:], rhs=H[:, c, tsl], start=(c == 0), stop=(c == 7))
                    return ins
                P.op("pe", mm, reads=["ONES"] + [("H", c, tb) for c in range(8)], writes=[("ps", banks[tb])])

            def finish(tb):
                tsl = slice(tb * 512, (tb + 1) * 512)
                P.op("act", lambda e: e.activation(out=RB[:, tsl], in_=bank(banks[tb]), func=AF.Ln,
                                                   bias=EPST[:, 0:1], scale=1.0 / D),
                     reads=["EPS"], writes=[("RB", tb), ("ps", banks[tb])])
                P.op("act", lambda e: e.activation(out=RB[:, tsl], in_=RB[:, tsl], func=AF.Exp, scale=-0.5),
                     reads=[("RB", tb)], writes=[("RB", tb)])
                for c in range(8):
                    dst = X if final else U
                    P.op("dve", lambda e, c=c, dst=dst: e.scalar_tensor_tensor(
                        out=dst[:, c, tsl], in0=X[:, c, tsl], scalar=PR[:, gcol + c:gcol + c + 1], in1=RB[:, tsl],
                        op0=ALU.mult, op1=ALU.mult),
                        reads=["PR", ("RB", tb), ("X", c, tb)], writes=[(("X" if final else "U"), c, tb)])
                if final:
                    P.dma("sp", lambda e, sem: e.dma_start(out=out_d[:, :, tsl], in_=X[:, :, tsl]).then_inc(sem, 16),
                          f"o{tb}", reads=[("X", c, tb) for c in range(8)], writes=[("OUT", tb)])

            for tb in range(4):
                stats(tb)
                if tb >= 1:
                    finish(tb - 1)
            finish(3)

        def ffn_plan(wi, wo):
            order = []
            for hp in range(2):
                order += [(wi[hp * NHP + cc], 2048) for cc in range(NHP)]
                order += [(wo[hp, m], NHP * 128) for m in range(8)]
            return order
        plan_pos = {}
        FFNREP = int(os.environ.get("FFNREP", "0"))
        if "ffn1" in stages:
            plan_pos["ffn1"] = len(wplan); wplan += ffn_plan(w1i_d, w1o_d)
            for rep in range(FFNREP):
                plan_pos["ffn1r%d" % rep] = len(wplan); wplan += ffn_plan(w1i_d, w1o_d)
        if "mix" in stages:
            plan_pos["qkv"] = len(wplan); wplan += [(wqkv_d[i], 2048) for i in range(12)]
            plan_pos["merge"] = len(wplan)
            for mg in range(2):
                for mm_ in range(4):
                    m = mg * 4 + mm_
                    wplan += [(wg_d[m], 2048), (wbr_d[m], 1024)]
                wplan += [(wo_d[mg, 0], 2048), (wo_d[mg, 1], 2048)]
        if "ffn2" in stages:
            plan_pos["ffn2"] = len(wplan); wplan += ffn_plan(w2i_d, w2o_d)

        def ffn_stage(key, gcol):
            base = plan_pos[key]
            rms_norm(gcol)
            for hp in range(2):
                pbase = base + hp * (NHP + 8)
                for cc in range(NHP):
                    s = wneed(pbase + cc)
                    for tb in range(4):
                        bg = nxt("ps", 7)
                        bu = nxt("ps", 7)
                        tsl = slice(tb * 512, (tb + 1) * 512)

                        def mm(e, s=s, bg=bg, bu=bu, tsl=tsl):
                            ins = None
                            for kc in range(8):
                                e.matmul(bank(bg), lhsT=WB[s][:, kc * 256:kc * 256 + 128], rhs=U[:, kc, tsl],
                                         start=(kc == 0), stop=(kc == 7))
                            for kc in range(8):
                                ins = e.matmul(bank(bu), lhsT=WB[s][:, kc * 256 + 128:kc * 256 + 256],
                                               rhs=U[:, kc, tsl], start=(kc == 0), stop=(kc == 7))
                            return ins
                        P.op("pe", mm, reads=[("WB", s)] + [("U", kc, tb) for kc in range(8)],
                             writes=[("ps", bg), ("ps", bu)])
                        sc = nxt("sc", NSC)
                        P.op("act", lambda e, bg=bg, sc=sc: e.activation(out=SC[sc][:], in_=bank(bg), func=AF.Silu),
                             writes=[("ps", bg), ("SC", sc)])
                        P.op("dve", lambda e, bu=bu, sc=sc, cc=cc, tsl=tsl: e.tensor_tensor(
                            out=H[:, cc, tsl], in0=bank(bu), in1=SC[sc][:], op=ALU.mult),
                            reads=[("SC", sc)], writes=[("ps", bu), ("H", cc, tb)])
                for m in range(8):
                    s = wneed(pbase + NHP + m)
                    for tb in range(4):
                        b = nxt("ps", 7)
                        tsl = slice(tb * 512, (tb + 1) * 512)

                        def mm(e, s=s, b=b, tsl=tsl):
                            ins = None
                            for kc in range(NHP):
                                ins = e.matmul(bank(b), lhsT=WB[s][:, kc * 128:(kc + 1) * 128], rhs=H[:, kc, tsl],
                                               start=(kc == 0), stop=(kc == NHP - 1))
                            return ins
                        P.op("pe", mm, reads=[("WB", s)] + [("H", kc, tb) for kc in range(NHP)],
                             writes=[("ps", b)])
                        P.op("dve", lambda e, b=b, m=m, tsl=tsl: e.scalar_tensor_tensor(
                            out=X[:, m, tsl], in0=bank(b), scalar=0.5, in1=X[:, m, tsl],
                            op0=ALU.mult, op1=ALU.add),
                            writes=[("ps", b), ("X", m, tb)])

        def proj_fm(s, col0, hidx, evac_alt):
            for tb in range(4):
                b = nxt("ps", 7)
                tsl = slice(tb * 512, (tb + 1) * 512)

                def mm(e, b=b, tsl=tsl):
                    ins = None
                    for kc in range(8):
                        ins = e.matmul(bank(b), lhsT=WB[s][:, kc * 256 + col0:kc * 256 + col0 + 128],
                                       rhs=U[:, kc, tsl], start=(kc == 0), stop=(kc == 7))
                    return ins
                P.op("pe", mm, reads=[("WB", s)] + [("U", kc, tb) for kc in range(8)], writes=[("ps", b)])
                if (tb + evac_alt) % 2 == 0:
                    P.op("act", lambda e, b=b, tsl=tsl: e.copy(out=H[:, hidx, tsl], in_=bank(b)),
                         writes=[("ps", b), ("H", hidx, tb)])
                else:
                    P.op("dve", lambda e, b=b, tsl=tsl: e.tensor_copy(out=H[:, hidx, tsl], in_=bank(b)),
                         writes=[("ps", b), ("H", hidx, tb)])

        def proj_v(s, nh, hd):
            for tt in range(16):
                b = nxt("ps", 7)

                def mm(e, b=b, tt=tt):
                    ins = None
                    for kc in range(8):
                        ins = e.matmul(bank(b)[:, 0:256], lhsT=U[:, kc, tt * 128:(tt + 1) * 128],
                                       rhs=WB[s][:, kc * 256:(kc + 1) * 256], start=(kc == 0), stop=(kc == 7))
                    return ins
                P.op("pe", mm, reads=[("WB", s)] + [("U", kc, tt // 4) for kc in range(8)], writes=[("ps", b)])
                P.op("dve", lambda e, b=b, tt=tt: e.tensor_copy(
                    out=V[:, tt, 0:nh * (hd + 1)].rearrange("p (h e) -> p h e", e=hd + 1)[:, :, 0:hd],
                    in_=bank(b)[:, 0:256].rearrange("p (h d) -> p h d", d=hd)),
                    writes=[("ps", b), ("V", tt)])

        def transpose_out(ys, nblk, hbase, tok0, reads):
            tp = nxt("tp", 2)
            toff = tp * 512

            def tr(e, ys=ys, toff=toff):
                ins = None
                for i in range(nblk):
                    ins = e.transpose(out=PST[:, toff + i * 128:toff + (i + 1) * 128],
                                      in_=YS[ys][:, i * 128:(i + 1) * 128], identity=IDb[:])
                return ins
            P.op("pe", tr, reads=["ID", ("YS", ys)], writes=["pst"])
            P.op("dve", lambda e, toff=toff: e.tensor_copy(
                out=H[:, hbase:hbase + nblk, tok0:tok0 + 128],
                in_=PST[:, toff:toff + nblk * 128].rearrange("p (c q) -> p c q", q=128)),
                writes=["pst"] + [("H", hbase + i, tok0 // 512) for i in range(nblk)])

        LA = int(os.environ.get('KLA', '2'))

        def run_pipelined(steps):
            n = len(steps)
            LAG2, LAG3 = int(os.environ.get('KLAG2', '4')), int(os.environ.get('KLAG3', '8'))
            starts = [i_ for i_, s_ in enumerate(steps) if s_.get("pre")]
            for k_ in range(len(starts) - 1, 0, -1):
                steps[starts[k_ - 1]]["pre_next"] = steps[starts[k_]].pop("pre")
            for idx in range(n + LA + LAG3):
                if idx < n:
                    stp = steps[idx]
                    if stp.get("pre"):
                        stp["pre"]()
                    if stp.get("pre_next"):
                        stp["pre_next"]()
                    stp["qk"]()
                k = idx - LA
                if 0 <= k < n:
                    stp = steps[k]
                    stp["pv"]()
                    if stp.get("post"):
                        stp["post"]()
                k2 = idx - LA - LAG2
                if 0 <= k2 < n and steps[k2].get("post2"):
                    steps[k2]["post2"]()
                k2b = idx - LA - LAG2 - 1
                if 0 <= k2b < n and steps[k2b].get("post2b"):
                    steps[k2b]["post2b"]()
                k3 = idx - LA - LAG3
                if 0 <= k3 < n and steps[k3].get("post3"):
                    steps[k3]["post3"]()

        def attn_A(g, qkv_base):
            P.dma("pool", lambda e, sem: e.dma_start(out=BI[:, 0:2048], in_=ba_d[g]).then_inc(sem, 16), "bi",
                  writes=["BI"])
            s = wneed(qkv_base + 0)
            proj_fm(s, 0, 8, 0); proj_fm(s, 128, 9, 1)
            s = wneed(qkv_base + 1)
            proj_fm(s, 0, 10, 0); proj_fm(s, 128, 11, 1)
            s = wneed(qkv_base + 2)
            proj_v(s, 4, 64)
            jtype = [0, 1, 1, 2, 3]
            steps = []
            for pr in range(16):
                qz = nxt("qz", 2)
                q0 = pr * 128
                bo = pr % 2
                js = [j for j in range(5) if pr - 4 + j >= 0]
                for j in js:
                    kt = pr - 4 + j
                    bs = 2 + nxt("psa", 5)
                    ty = jtype[j]
                    ei = nxt("e", NE)
                    stp = {}
                    if j == js[0]:
                        def pre(qz=qz, q0=q0, pr=pr):
                            for half in range(2):
                                psl = slice(half * 64, half * 64 + 64)
                                P.op("dve", lambda e, psl=psl, half=half: e.tensor_scalar(
                                    out=QZ[qz][psl, 0:512].rearrange("p (c h q) -> p c h q", c=2, h=2)[:, :, half, :],
                                    in0=H[psl, 8:10, q0:q0 + 128], scalar1=0.125, scalar2=None, op0=ALU.mult),
                                    reads=[("H", 8, pr // 4), ("H", 9, pr // 4)], writes=[("QZ", qz, half)])
                        stp["pre"] = pre

                    def qk(bs=bs, ty=ty, kt=kt, qz=qz, ei=ei):
                        def mm(e):
                            e.matmul(bank(bs), lhsT=IDb[:], rhs=BI[:, ty * 512:(ty + 1) * 512], start=True, stop=False)
                            ins = None
                            for hl in range(4):
                                ins = e.matmul(bank(bs)[:, hl * 128:(hl + 1) * 128],
                                               lhsT=H[:, 10 + hl // 2, kt * 128:(kt + 1) * 128],
                                               rhs=QZ[qz][:, hl * 128:(hl + 1) * 128], start=False, stop=(hl == 3))
                            return ins
                        P.op("pe", mm, reads=["ID", "BI", ("QZ", qz, 0), ("QZ", qz, 1), ("H", 10, kt // 4),
                                              ("H", 11, kt // 4)], writes=[("ps", bs)])
                        P.op("act", lambda e: e.activation(out=E[ei][:], in_=bank(bs), func=AF.Exp),
                             writes=[("ps", bs), ("E", ei)])
                    stp["qk"] = qk

                    def pv(ei=ei, kt=kt, bo=bo, first=(j == js[0]), last=(j == js[-1])):
                        def mm(e):
                            ins = None
                            for hl in range(4):
                                ins = e.matmul(bank(bo)[:, hl * 65:(hl + 1) * 65], lhsT=E[ei][:, hl * 128:(hl + 1) * 128],
                                               rhs=V[:, kt, hl * 65:(hl + 1) * 65], start=(first and hl == 0),
                                               stop=last, skip_group_check=True)
                            return ins
                        P.op("pe", mm, reads=[("E", ei), ("V", kt)], writes=[("ps", bo)])
                    stp["pv"] = pv
                    if j == js[-1]:
                        sm = nxt("st", NST)
                        ys = nxt("ys", NYS)

                        def post(bo=bo, sm=sm, ys=ys):
                            P.op("dve", lambda e: e.reciprocal(
                                out=SM[sm][:, 0:4],
                                in_=bank(bo)[:, 0:260].rearrange("p (h e) -> p h e", e=65)[:, :, 64]),
                                writes=[("ps", bo), ("SM", sm)])
                            P.op("dve", lambda e: e.tensor_tensor(
                                out=YS[ys][:, 0:256].rearrange("p (h d) -> p h d", d=64),
                                in0=bank(bo)[:, 0:260].rearrange("p (h e) -> p h e", e=65)[:, :, 0:64],
                                in1=SM[sm][:, 0:4].unsqueeze(2).to_broadcast([128, 4, 64]), op=ALU.mult),
                                reads=[("SM", sm)], writes=[("ps", bo), ("YS", ys)])
                        stp["post"] = post

                        def post2(ys=ys, q0=q0):
                            transpose_out(ys, 2, 2 * g, q0, None)
                        stp["post2"] = post2
                    steps.append(stp)
            run_pipelined(steps)

        def attn_B(g, qkv_base):
            P.dma("pool", lambda e, sem: e.dma_start(out=BI[:, 0:1280], in_=bb_d[g]).then_inc(sem, 16), "bi",
                  writes=["BI"])
            for hh_ in range(2):
                P.op("dve", lambda e, hh_=hh_: e.tensor_scalar(
                    out=BI[:, hh_ * 640:(hh_ + 1) * 640], in0=BI[:, hh_ * 640:(hh_ + 1) * 640],
                    scalar1=PR[:, PC_T5C + 2 * g + hh_:PC_T5C + 2 * g + hh_ + 1], scalar2=None, op0=ALU.subtract),
                    reads=["BI", "PR"], writes=["BI"])
            s = wneed(qkv_base + 0)
            proj_fm(s, 0, 8, 0); proj_fm(s, 128, 9, 1)
            s = wneed(qkv_base + 1)
            proj_fm(s, 0, 10, 0); proj_fm(s, 128, 11, 1)
            s = wneed(qkv_base + 2)
            proj_v(s, 2, 128)
            steps = []
            for hh in range(2):
                h = 2 * g + hh
                for QB in range(8):
                    qz = nxt("qz", 2)
                    accb = 2 * nxt("accb", 2)
                    t0 = QB * 256
                    for j in range(2 * QB + 2):
                        i0 = max(j, 2 * QB)
                        nq = 2 * QB + 2 - i0
                        N = nq * 128
                        qoff = (i0 - 2 * QB) * 128
                        near = (i0 - j) <= 1
                        if True:
                            bs = 4 + nxt("psb", 3)
                            ei = nxt("e", NE)
                            stp = {}
                            if j == 0:
                                def pre(qz=qz, t0=t0, hh=hh, QB=QB):
                                    for rr in range(2):
                                        psl = slice(rr * 64, rr * 64 + 64)
                                        P.op("dve", lambda e, psl=psl, rr=rr: e.tensor_scalar(
                                            out=QZ[qz][psl, rr * 256:(rr + 1) * 256], in0=H[psl, 8 + hh, t0:t0 + 256],
                                            scalar1=0.125, scalar2=None, op0=ALU.mult),
                                            reads=[("H", 8 + hh, QB // 2)], writes=[("QZ", qz, rr)])
                                stp["pre"] = pre

                            def qk(bs=bs, ei=ei, near=near, N=N, i0=i0, j=j, qz=qz, qoff=qoff, hh=hh, h=h):
                                def mm(e):
                                    ins = None
                                    for r in range(2):
                                        o_ = bank(bs)[:, r * 256:r * 256 + N]
                                        if near:
                                            boff = hh * 640 + (i0 - j) * 128
                                            e.matmul(o_, lhsT=IDb[:], rhs=BI[:, boff:boff + N],
                                                     start=(r == 0), stop=False, skip_group_check=True)
                                        ins = e.matmul(o_, lhsT=H[:, 10 + hh, j * 128:(j + 1) * 128],
                                                       rhs=QZ[qz][:, r * 256 + qoff:r * 256 + qoff + N],
                                                       start=(r == 0 and not near), stop=True, skip_group_check=True)
                                    return ins
                                P.op("pe", mm, reads=["ID", "BI", ("QZ", qz, 0), ("QZ", qz, 1), ("H", 10 + hh, j // 4)],
                                     writes=[("ps", bs)])
                                src_ = bank(bs).rearrange("p (r q) -> p r q", r=2)[:, :, 0:N]
                                dst_ = E[ei][:].rearrange("p (r q) -> p r q", r=2)[:, :, 0:N]
                                P.op("act", lambda e: e.activation(out=dst_, in_=src_, func=AF.Exp),
                                     writes=[("ps", bs), ("E", ei)])
                            stp["qk"] = qk

                            def pv(ei=ei, nq=nq, i0=i0, j=j, QB=QB, hh=hh, accb=accb):
                                def mm(e):
                                    ins = None
                                    for ii in range(nq):
                                        i = i0 + ii
                                        bi_ = accb + i - 2 * QB
                                        for r in range(2):
                                            ins = e.matmul(bank(bi_)[:, r * 129:(r + 1) * 129],
                                                           lhsT=E[ei][:, r * 256 + ii * 128:r * 256 + (ii + 1) * 128],
                                                           rhs=V[:, j, hh * 129:(hh + 1) * 129],
                                                           start=(j == 0 and r == 0), stop=(j == i and r == 1),
                                                           skip_group_check=True)
                                    return ins
                                P.op("pe", mm, reads=[("E", ei), ("V", j)],
                                     writes=[("ps", accb + i0 + ii - 2 * QB) for ii in range(nq)])
                            stp["pv"] = pv
                            r = 1
                            if j >= 2 * QB and r == 1:
                                sg = nxt("st", NST)
                                ys = nxt("ys", NYS)

                                def post(i=j, QB=QB, sg=sg, accb=accb):
                                    bi_ = accb + i - 2 * QB
                                    P.op("dve", lambda e: e.tensor_copy(out=STG[sg][:], in_=bank(bi_)[:, 0:258]),
                                         writes=[("ps", bi_), ("STG", sg)])
                                    P.op("dve", lambda e: e.reciprocal(
                                        out=SM[sg][:, 0:2],
                                        in_=STG[sg][:, 0:258].rearrange("p (r e) -> p r e", e=129)[:, :, 128]),
                                        reads=[("STG", sg)], writes=[("SM", sg)])
                                    P.op("dve", lambda e: e.tensor_scalar(
                                        out=STG[sg][:, 129:257], in0=STG[sg][:, 129:257], scalar1=SM[sg][:, 1:2],
                                        scalar2=LAMT[:, 2:3], op0=ALU.mult, op1=ALU.mult),
                                        reads=[("STG", sg), ("SM", sg), "LAM"], writes=[("STG", sg)])
                                    P.op("dve", lambda e: e.scalar_tensor_tensor(
                                        out=STG[sg][:, 0:128], in0=STG[sg][:, 0:128], scalar=SM[sg][:, 0:1],
                                        in1=STG[sg][:, 129:257], op0=ALU.mult, op1=ALU.add),
                                        reads=[("STG", sg), ("SM", sg)], writes=[("STG", sg)])
                                    P.op("dve", lambda e: e.scalar_tensor_tensor(
                                        out=STG[sg][:, 129:257], in0=STG[sg][:, 0:128], scalar=1.0,
                                        in1=STG[sg][:, 0:128], op0=ALU.mult, op1=ALU.mult, accum_out=SM[sg][:, 2:3]),
                                        reads=[("STG", sg)], writes=[("STG", sg), ("SM2", sg)])
                                stp["post"] = post

                                def post2(i=j, h=h, sg=sg, ys=ys):
                                    P.op("act", lambda e: e.activation(
                                        out=SM[sg][:, 3:4], in_=SM[sg][:, 2:3], func=AF.Ln, bias=EPST[:, 0:1],
                                        scale=1.0 / 128),
                                        reads=[("SM2", sg), "EPS"], writes=[("SM3", sg)])
                                stp["post2"] = post2

                                def post2b(i=j, h=h, sg=sg, ys=ys):
                                    P.op("act", lambda e: e.activation(
                                        out=SM[sg][:, 4:5], in_=SM[sg][:, 3:4], func=AF.Exp, scale=-0.5),
                                        reads=[("SM3", sg)], writes=[("SM4", sg)])
                                    P.op("dve", lambda e: e.scalar_tensor_tensor(
                                        out=YS[ys][:, 0:128], in0=STG[sg][:, 0:128], scalar=SM[sg][:, 4:5], in1=G08[:],
                                        op0=ALU.mult, op1=ALU.mult),
                                        reads=[("STG", sg), ("SM4", sg), "G08"], writes=[("YS", ys)])
                                stp["post2b"] = post2b

                                def post3(i=j, h=h, ys=ys):
                                    transpose_out(ys, 1, 4 + h, i * 128, None)
                                stp["post3"] = post3
                            steps.append(stp)
            run_pipelined(steps)

        WOC = [0]

        def mix_stage():
            base = plan_pos["qkv"]
            rms_norm(PC_GM)
            P.op("dve", lambda e: e.scalar_tensor_tensor(
                out=LTMP[:], in0=PR[:, PC_LQ1:PC_LQ1 + 64], scalar=1.0, in1=PR[:, PC_LK1:PC_LK1 + 64],
                op0=ALU.mult, op1=ALU.mult, accum_out=LAMT[:, 0:1]), reads=["PR"], writes=["LTMP", "LAM0"])
            P.op("dve", lambda e: e.scalar_tensor_tensor(
                out=LTMP[:], in0=PR[:, PC_LQ2:PC_LQ2 + 64], scalar=1.0, in1=PR[:, PC_LK2:PC_LK2 + 64],
                op0=ALU.mult, op1=ALU.mult, accum_out=LAMT[:, 1:2]), reads=["PR"], writes=["LTMP", "LAM1"])
            P.op("act", lambda e: e.activation(out=LAMT[:, 4:6], in_=LAMT[:, 0:2], func=AF.Exp),
                 reads=["LAM0", "LAM1"], writes=["LAME"])
            P.op("dve", lambda e: e.tensor_tensor(out=LAMT[:, 3:4], in0=LAMT[:, 5:6], in1=LAMT[:, 4:5],
                                                  op=ALU.subtract), reads=["LAME"], writes=["LAMD"])
            P.op("dve", lambda e: e.tensor_scalar(out=LAMT[:, 2:3], in0=LAMT[:, 3:4], scalar1=-LAMBDA_INIT,
                                                  scalar2=None, op0=ALU.add), reads=["LAMD"], writes=["LAM"])
            P.op("dve", lambda e: e.tensor_scalar(out=G08[:], in0=PR[:, PC_SUBG:PC_SUBG + 128],
                                                  scalar1=1.0 - LAMBDA_INIT, scalar2=None, op0=ALU.mult),
                 reads=["PR"], writes=["G08"])
            if MIXDBG == -1:
                pass
            P.op("dve", lambda e: e.memset(V[:], 1.0), writes=[("V", tt) for tt in range(16)])
            for q in range(2):
                P.op("dve", lambda e, q=q: e.memset(QZ[q][:], 0.0), writes=[("QZ", q, 0), ("QZ", q, 1)])
            if 0 <= MIXDBG < 2:
                return
            for g in range(2 if MIXDBG >= 0 else 0):
                attn_A(g, base + 3 * g)
                if MIXDBG < 5:
                    return
            P.op("dve", lambda e: e.memset(V[:], 1.0), writes=[("V", tt) for tt in range(16)])
            for q in range(2):
                P.op("dve", lambda e, q=q: e.memset(QZ[q][:], 0.0), writes=[("QZ", q, 0), ("QZ", q, 1)])
            if 0 <= MIXDBG < 6:
                return
            for g in range(2 if MIXDBG >= 0 else 0):
                attn_B(g, base + 6 + 3 * g)
                if MIXDBG < 9:
                    return
            if 0 <= MIXDBG < 10:
                return
            mb = plan_pos["merge"]
            for mg in range(2):
                gb = mb + mg * 10
                for mm_ in range(4 if MIXDBG >= 0 else 0):
                    m = mg * 4 + mm_
                    sgw = wneed(gb + 2 * mm_)
                    sbr = wneed(gb + 2 * mm_ + 1)
                    for tb in range(4):
                        tsl = slice(tb * 512, (tb + 1) * 512)
                        bc, bd, ba, bb_ = nxt("ps", 7), nxt("ps", 7), nxt("ps", 7), nxt("ps", 7)

                        def mmg(e, sgw=sgw, tsl=tsl, bc=bc, bd=bd):
                            ins = None
                            for kc in range(8):
                                e.matmul(bank(bc), lhsT=WB[sgw][:, kc * 256:kc * 256 + 128], rhs=U[:, kc, tsl],
                                         start=(kc == 0), stop=(kc == 7))
                            for kc in range(8):
                                ins = e.matmul(bank(bd), lhsT=WB[sgw][:, kc * 256 + 128:kc * 256 + 256],
                                               rhs=U[:, kc, tsl], start=(kc == 0), stop=(kc == 7))
                            return ins
                        P.op("pe", mmg, reads=[("WB", sgw)] + [("U", kc, tb) for kc in range(8)],
                             writes=[("ps", bc), ("ps", bd)])

                        def mmp(e, sbr=sbr, tsl=tsl, ba=ba, bb_=bb_):
                            for kc in range(4):
                                e.matmul(bank(ba), lhsT=WB[sbr][:, kc * 128:(kc + 1) * 128], rhs=H[:, kc, tsl],
                                         start=(kc == 0), stop=(kc == 3))
                            ins = None
                            for kc in range(4):
                                ins = e.matmul(bank(bb_), lhsT=WB[sbr][:, 512 + kc * 128:512 + (kc + 1) * 128],
                                               rhs=H[:, 4 + kc, tsl], start=(kc == 0), stop=(kc == 3))
                            return ins
                        P.op("pe", mmp, reads=[("WB", sbr)] + [("H", kc, tb) for kc in range(8)],
                             writes=[("ps", ba), ("ps", bb_)])
                        s1, s2 = nxt("sc", NSC), nxt("sc", NSC)
                        P.op("act", lambda e, bc=bc, s1=s1, m=m: e.activation(
                            out=SC[s1][:], in_=bank(bc), func=AF.Sigmoid, bias=PR[:, PC_BG + m:PC_BG + m + 1]),
                            reads=["PR"], writes=[("ps", bc), ("SC", s1)])
                        P.op("act", lambda e, bd=bd, s2=s2, m=m: e.activation(
                            out=SC[s2][:], in_=bank(bd), func=AF.Sigmoid, bias=PR[:, PC_BG + 8 + m:PC_BG + 9 + m]),
                            reads=["PR"], writes=[("ps", bd), ("SC", s2)])
                        P.op("dve", lambda e, ba=ba, s1=s1: e.tensor_tensor(
                            out=SC[s1][:], in0=bank(ba), in1=SC[s1][:], op=ALU.mult),
                            reads=[("SC", s1)], writes=[("ps", ba), ("SC", s1)])
                        P.op("dve", lambda e, bb_=bb_, s2=s2: e.tensor_tensor(
                            out=SC[s2][:], in0=bank(bb_), in1=SC[s2][:], op=ALU.mult),
                            reads=[("SC", s2)], writes=[("ps", bb_), ("SC", s2)])
                        P.op("dve", lambda e, s1=s1, s2=s2, mm_=mm_, tsl=tsl: e.tensor_tensor(
                            out=H[:, 8 + mm_, tsl], in0=SC[s1][:], in1=SC[s2][:], op=ALU.add),
                            reads=[("SC", s1), ("SC", s2)], writes=[("H", 8 + mm_, tb)])
                if MIXDBG < 12 and MIXDBG >= 0:
                    return
                sos = [wneed(gb + 8), wneed(gb + 9)]
                for tb in range(4):
                    for half in range(2):
                        so = sos[half]
                        for m4 in range(4):
                            mo = half * 4 + m4
                            tsl = slice(tb * 512, (tb + 1) * 512)
                            b = nxt("ps", 7)

                            def mm(e, so=so, m4=m4, tsl=tsl, b=b):
                                ins = None
                                for kc in range(4):
                                    ins = e.matmul(bank(b), lhsT=WB[so][:, m4 * 512 + kc * 128:m4 * 512 + (kc + 1) * 128],
                                                   rhs=H[:, 8 + kc, tsl], start=(kc == 0), stop=(kc == 3))
                                return ins
                            P.op("pe", mm, reads=[("WB", so)] + [("H", 8 + kc, tb) for kc in range(4)],
                                 writes=[("ps", b)])
                            P.op("dve", lambda e, b=b, mo=mo, tsl=tsl: e.scalar_tensor_tensor(
                                out=X[:, mo, tsl], in0=bank(b), scalar=1.0, in1=X[:, mo, tsl],
                                op0=ALU.mult, op1=ALU.add),
                                writes=[("ps", b), ("X", mo, tb)])
                if MIXDBG < 14:
                    return

        if "ffn1" in stages:
            ffn_stage("ffn1", PC_G1)
            for rep in range(FFNREP):
                ffn_stage("ffn1r%d" % rep, PC_G1)
        if "mix" in stages:
            mix_stage()
        if "ffn2" in stages:
            ffn_stage("ffn2", PC_G2)
        rms_norm(PC_GF, final=True)
        P.op("sp", lambda e: None, reads=[("OUT", tb) for tb in range(4)])
        P.emit_all(nc, st)
    return nc


def _t5_bucket_np(rel):
    nb = 16
    ret = np.where(rel > 0, nb, 0)
    n = np.abs(rel)
    max_exact = nb // 2
    nf = np.maximum(n, 1).astype(np.float32)
    large = max_exact + (np.log(nf / np.float32(max_exact)) / np.float32(math.log(128 / max_exact))
                         * np.float32(nb - max_exact)).astype(np.int32)
    large = np.minimum(large, nb - 1)
    return ret + np.where(n < max_exact, n, large)


def _kc_tile(w_cols):
    n = w_cols.shape[1]
    return np.ascontiguousarray(w_cols.reshape(8, 128, n).transpose(1, 0, 2)).reshape(128, 8 * n)


def _prep_shared(inp):
    f = lambda a: np.asarray(a, dtype=np.float32)
    sh = {}

    def ffn_w(w_in, w_out):
        w_in = f(w_in)[0]
        w_out = f(w_out)[0]
        wi = np.empty((NCH, 128, 2048), np.float32)
        for c in range(NCH):
            cols = np.concatenate([w_in[:, c * 128:(c + 1) * 128], w_in[:, DFF + c * 128:DFF + (c + 1) * 128]], axis=1)
            wi[c] = _kc_tile(cols)
        wo = w_out.reshape(2, NHP, 128, 8, 128).transpose(0, 3, 2, 1, 4)
        wo = np.ascontiguousarray(wo).reshape(2, 8, 128, NHP * 128)
        return wi, wo

    sh["w1i"], sh["w1o"] = ffn_w(inp["ffn1_w_in"], inp["ffn1_w_out"])
    sh["w2i"], sh["w2o"] = ffn_w(inp["ffn2_w_in"], inp["ffn2_w_out"])
    wm = f(inp["w_mix_in"])[0]
    qkv = []
    for (qo, ko, vo) in ((0, 512, 1024), (1536, 2048, 2560)):
        for g in range(2):
            for off in (qo, ko, vo):
                qkv.append(_kc_tile(wm[:, off + g * 256:off + (g + 1) * 256]))
    sh["wqkv"] = np.stack(qkv)
    sh["wg"] = np.stack([_kc_tile(np.concatenate([wm[:, 3072 + m * 128:3072 + (m + 1) * 128],
                                                  wm[:, 4096 + m * 128:4096 + (m + 1) * 128]], axis=1))
                         for m in range(8)])
    wa = f(inp["w_branch_a"])[0]
    wb = f(inp["w_branch_b"])[0]

    def br_tile(w, m):
        return np.ascontiguousarray(w[:, m * 128:(m + 1) * 128].reshape(4, 128, 128).transpose(1, 0, 2)).reshape(128, 512)
    sh["wbr"] = np.stack([np.concatenate([br_tile(wa, m), br_tile(wb, m)], axis=1) for m in range(8)])
    wo = f(inp["w_o"])[0]
    wo_t = np.empty((2, 2, 128, 2048), np.float32)
    for mg in range(2):
        for half in range(2):
            blk = wo[mg * 512:(mg + 1) * 512, half * 512:(half + 1) * 512]
            t = blk.reshape(4, 128, 4, 128).transpose(1, 2, 0, 3)
            wo_t[mg, half] = np.ascontiguousarray(t).reshape(128, 2048)
    sh["wo"] = wo_t

    par = np.zeros((128, NPAR), np.float32)
    for name, col in (("ffn1_norm", PC_G1), ("mix_norm", PC_GM), ("ffn2_norm", PC_G2)):
        par[:, col:col + 8] = f(inp[name])[0].reshape(8, 128).T
    par[:, PC_GF:PC_GF + 8] = f(inp["final_norm"]).reshape(8, 128).T
    par[:, PC_BG:PC_BG + 16] = f(inp["b_gate"])[0].reshape(16, 128).T
    for name, col in (("lambda_q1", PC_LQ1), ("lambda_k1", PC_LK1), ("lambda_q2", PC_LQ2), ("lambda_k2", PC_LK2)):
        par[:, col:col + 64] = f(inp[name])[0][None, :]
    par[:, PC_SUBG:PC_SUBG + 128] = f(inp["subln_g"])[0][None, :]
    t5 = f(inp["t5_bias"])
    par[:, PC_T5C:PC_T5C + 4] = t5[:, 15][None, :]
    sh["params"] = par
    sh["ident"] = np.eye(128, dtype=np.float32)

    rel_tab = f(inp["rel_bias_a"])[0]
    kk = np.arange(128)[:, None]
    qq = np.arange(128)[None, :]
    idx3 = np.clip(-128 + kk - qq, -128, 128) + 128
    idx4 = np.clip(kk - qq, -128, 128) + 128
    mask0 = (kk < 64) & (qq >= 64)
    mask4 = (kk >= 64) & (qq < 64)
    bA = np.empty((2, 128, 4, 4, 128), np.float32)
    for h in range(8):
        g, hl = divmod(h, 4)
        c0 = np.broadcast_to(rel_tab[h, 0], (128, 128))
        bA[g, :, 0, hl, :] = np.where(mask0, np.float32(MASKV), c0)
        bA[g, :, 1, hl, :] = c0
        bA[g, :, 2, hl, :] = rel_tab[h][idx3]
        bA[g, :, 3, hl, :] = np.where(mask4, np.float32(MASKV), rel_tab[h][idx4])
    sh["biasA"] = bA.reshape(2, 128, 2048)
    bD = _t5_bucket_np(kk - qq)
    bS = _t5_bucket_np(kk - qq - 128)
    bB = np.empty((2, 128, 2, 5, 128), np.float32)
    for h in range(4):
        g, hh = divmod(h, 2)
        bB[g, :, hh, 0, :] = np.where(mask4, np.float32(MASKV), t5[h][bD])
        bB[g, :, hh, 1, :] = t5[h][bS]
        bB[g, :, hh, 2:5, :] = t5[h, 15]
    sh["biasB"] = bB.reshape(2, 128, 1280)
    return sh


_NC_CACHE = {}


def _run(inp, stages=("ffn1", "mix", "ffn2"), ncores=8):
    x = np.asarray(inp["x"], dtype=np.float32)
    sh = _prep_shared(inp)
    key = tuple(stages)
    if key not in _NC_CACHE:
        _NC_CACHE[key] = build_program(stages)
    nc = _NC_CACHE[key]
    in_maps = []
    for b in range(ncores):
        xT = np.ascontiguousarray(x[b].T.reshape(8, 128, S).transpose(1, 0, 2))
        m = {"xT": xT}
        m.update(sh)
        in_maps.append(m)
    res = run_bass_kernel_spmd(nc, in_maps, core_ids=list(range(ncores)))
    outs = []
    for b in range(ncores):
        oT = np.asarray(res.results[b]["outT"])
        outs.append(oT.transpose(2, 1, 0).reshape(S, D))
    return np.stack(outs).astype(np.float32)


def kernel(**inputs):
    return _run(inputs)
```
